# Optimizing a Trainium2 kernel written in Bass

```python
import math, functools
import jax, jax.numpy as jnp
from jax import lax
import numpy as np

D_MODEL = 1024
BATCH = 8
SEQ = 2048
DEPTH = 4
DEC_BATCH = 128
DEC_SEQ = 4
PAST_LEN = 8192
PAGE_SIZE = 128

ATTN_WIDTH = D_MODEL // 2
HEAD_DIM = 64
N_HEADS = ATTN_WIDTH // HEAD_DIM
N_KV_HEADS = 2
GROUP = N_HEADS // N_KV_HEADS
KV_WIDTH = N_KV_HEADS * HEAD_DIM
WINDOW = 128
POOL_WIDTH = D_MODEL - ATTN_WIDTH
POOL_WINDOWS = (2, 4, 8, 16)
N_POOL_GROUPS = len(POOL_WINDOWS)
POOL_GROUP_WIDTH = POOL_WIDTH // N_POOL_GROUPS
POOL_STATE = max(POOL_WINDOWS) - 1
IN_WIDTH = ATTN_WIDTH + 2 * KV_WIDTH + POOL_WIDTH
MIX_WIDTH = ATTN_WIDTH + POOL_WIDTH
D_FF = ((8 * D_MODEL + 3 * 256 - 1) // (3 * 256)) * 256
PLE_DIM = 256
EPS = 1e-6

kernel_name = "hymba_pool_swa_sink_decoder_step"


def _rms(x, g):
    xf = x.astype(jnp.float32)
    y = xf * lax.rsqrt(jnp.mean(xf * xf, axis=-1, keepdims=True) + EPS)
    return (y * g.astype(jnp.float32)).astype(x.dtype)


def _sink_attention(q, k, v, mask, sinks):
    s = jnp.einsum('bnqhgd,bnkhd->bnhgqk', q, k).astype(jnp.float32) * (HEAD_DIM ** -0.5)
    s = jnp.where(mask[None, :, None, None], s, -jnp.inf)
    sink = sinks.astype(jnp.float32)[None, None, :, :, None, None]
    m = jnp.maximum(jnp.max(s, axis=-1, keepdims=True), sink)
    p = jnp.exp(s - m)
    denom = jnp.sum(p, axis=-1, keepdims=True) + jnp.exp(sink - m)
    w = (p / denom).astype(v.dtype)
    return jnp.einsum('bnhgqk,bnkhd->bnqhgd', w, v)


def _swa_prompt(q, k, v, sinks):
    B, S, _ = q.shape
    nb = S // WINDOW
    qb = q.reshape(B, nb, WINDOW, N_KV_HEADS, GROUP, HEAD_DIM)
    k = k.reshape(B, S, N_KV_HEADS, HEAD_DIM)
    v = v.reshape(B, S, N_KV_HEADS, HEAD_DIM)
    pad = jnp.zeros((B, WINDOW, N_KV_HEADS, HEAD_DIM), k.dtype)
    kp = jnp.concatenate([pad, k], axis=1)[:, :S]
    vp = jnp.concatenate([pad, v], axis=1)[:, :S]
    kb = jnp.concatenate([kp.reshape(B, nb, WINDOW, N_KV_HEADS, HEAD_DIM),
                          k.reshape(B, nb, WINDOW, N_KV_HEADS, HEAD_DIM)], axis=2)
    vb = jnp.concatenate([vp.reshape(B, nb, WINDOW, N_KV_HEADS, HEAD_DIM),
                          v.reshape(B, nb, WINDOW, N_KV_HEADS, HEAD_DIM)], axis=2)
    a = jnp.arange(WINDOW)[:, None]
    c = jnp.arange(2 * WINDOW)[None, :]
    d = a + WINDOW - c
    blk = jnp.arange(nb)[:, None, None]
    mask = (d >= 0)[None] & (d < WINDOW)[None] & (blk * WINDOW + c[None] - WINDOW >= 0)
    o = _sink_attention(qb, kb, vb, mask, sinks).reshape(B, S, ATTN_WIDTH)
    return o, k[:, -WINDOW:], v[:, -WINDOW:]


def _swa_sample(q, k, v, k_buf, v_buf, sinks):
    B, S, _ = q.shape
    qb = q.reshape(B, 1, S, N_KV_HEADS, GROUP, HEAD_DIM)
    k_ext = jnp.concatenate([k_buf, k.reshape(B, S, N_KV_HEADS, HEAD_DIM)], axis=1)
    v_ext = jnp.concatenate([v_buf, v.reshape(B, S, N_KV_HEADS, HEAD_DIM)], axis=1)
    j = jnp.arange(S)[:, None]
    c = jnp.arange(WINDOW + S)[None, :]
    d = j + WINDOW - c
    mask = ((d >= 0) & (d < WINDOW))[None]
    o = _sink_attention(qb, k_ext[:, None], v_ext[:, None], mask, sinks).reshape(B, S, ATTN_WIDTH)
    return o, k_ext[:, -WINDOW:], v_ext[:, -WINDOW:]


def _pool_mixer(u, prev, pos0, w_pool, pool_scale):
    B, S, C = u.shape
    P = prev.shape[1]
    ext = jnp.concatenate([prev, u], axis=1)
    cs = jnp.concatenate([jnp.zeros((B, 1, C), jnp.float32),
                          jnp.cumsum(ext.astype(jnp.float32), axis=1)], axis=1)
    pos = pos0 + jnp.arange(S)
    parts = []
    for g, w in enumerate(POOL_WINDOWS):
        sl = slice(g * POOL_GROUP_WIDTH, (g + 1) * POOL_GROUP_WIDTH)
        tot = cs[:, P + 1:P + 1 + S, sl] - cs[:, P + 1 - w:P + 1 - w + S, sl]
        cnt = jnp.minimum(w, pos + 1).astype(jnp.float32)
        parts.append(tot / cnt[None, :, None])
    mean = jnp.concatenate(parts, axis=-1)
    dlt = (mean - u.astype(jnp.float32)).astype(u.dtype).reshape(B, S, N_POOL_GROUPS, POOL_GROUP_WIDTH)
    y = jnp.einsum('bsgc,gcd->bsgd', dlt, w_pool).reshape(B, S, C) * pool_scale
    return y, ext[:, -POOL_STATE:]


def _layer(x, p, attn_fn, pool_prev, pos0, n_mix_pre, n_mix_post, n_ffn_pre, n_ffn_post,
           w_in, w_out, w_pool, pool_scale, w_gate, w_up, w_down, w_ple, w_ple_gate):
    h = _rms(x, n_mix_pre)
    z = h @ w_in
    q = z[..., :ATTN_WIDTH]
    k = z[..., ATTN_WIDTH:ATTN_WIDTH + KV_WIDTH]
    v = z[..., ATTN_WIDTH + KV_WIDTH:ATTN_WIDTH + 2 * KV_WIDTH]
    u = z[..., ATTN_WIDTH + 2 * KV_WIDTH:]
    a, k_state, v_state = attn_fn(q, k, v)
    m, pool_state = _pool_mixer(u, pool_prev, pos0, w_pool, pool_scale)
    mix = jnp.concatenate([a, m], axis=-1) @ w_out
    x = x + _rms(mix, n_mix_post)
    f = _rms(x, n_ffn_pre)
    f = (jax.nn.silu(f @ w_gate) * (f @ w_up)) @ w_down
    x = x + _rms(f, n_ffn_post)
    x = x + jax.nn.sigmoid(x @ w_ple_gate) * (p @ w_ple)
    return x, k_state, v_state, pool_state


def setup_inputs(seed: int = 0) -> dict:
    key = jax.random.key(seed)
    ks = jax.random.split(key, 24)
    f32 = jnp.float32

    def nrm(k, shape, scale):
        return jax.random.normal(k, shape, f32) * scale

    def gain(k, shape):
        return 1.0 + 0.1 * jax.random.normal(k, shape, f32)

    return {
        "x_prompt": nrm(ks[0], (BATCH, SEQ, D_MODEL), 1.0),
        "x_sample": nrm(ks[1], (DEC_BATCH, DEC_SEQ, D_MODEL), 1.0),
        "p_prompt": nrm(ks[2], (DEPTH, BATCH, SEQ, PLE_DIM), 1.0),
        "p_sample": nrm(ks[3], (DEPTH, DEC_BATCH, DEC_SEQ, PLE_DIM), 1.0),
        "cache_k": nrm(ks[4], (DEPTH, DEC_BATCH, WINDOW, N_KV_HEADS, HEAD_DIM), 1.0),
        "cache_v": nrm(ks[5], (DEPTH, DEC_BATCH, WINDOW, N_KV_HEADS, HEAD_DIM), 1.0),
        "state_pool": nrm(ks[6], (DEPTH, DEC_BATCH, POOL_STATE, POOL_WIDTH), 1.0),
        "norm_mix_pre": gain(ks[7], (DEPTH, D_MODEL)),
        "norm_mix_post": gain(ks[8], (DEPTH, D_MODEL)),
        "norm_ffn_pre": gain(ks[9], (DEPTH, D_MODEL)),
        "norm_ffn_post": gain(ks[10], (DEPTH, D_MODEL)),
        "w_in": nrm(ks[11], (DEPTH, D_MODEL, IN_WIDTH), D_MODEL ** -0.5),
        "w_out": nrm(ks[12], (DEPTH, MIX_WIDTH, D_MODEL), MIX_WIDTH ** -0.5),
        "attn_sinks": nrm(ks[13], (DEPTH, N_HEADS), 0.5),
        "w_pool": nrm(ks[14], (DEPTH, N_POOL_GROUPS, POOL_GROUP_WIDTH, POOL_GROUP_WIDTH), POOL_GROUP_WIDTH ** -0.5),
        "pool_scale": gain(ks[15], (DEPTH, POOL_WIDTH)),
        "w_gate": nrm(ks[16], (DEPTH, D_MODEL, D_FF), D_MODEL ** -0.5),
        "w_up": nrm(ks[17], (DEPTH, D_MODEL, D_FF), D_MODEL ** -0.5),
        "w_down": nrm(ks[18], (DEPTH, D_FF, D_MODEL), D_FF ** -0.5),
        "w_ple": nrm(ks[19], (DEPTH, PLE_DIM, D_MODEL), PLE_DIM ** -0.5),
        "w_ple_gate": nrm(ks[20], (DEPTH, D_MODEL, D_MODEL), D_MODEL ** -0.5),
    }


def reference(x_prompt, x_sample, p_prompt, p_sample, cache_k, cache_v, state_pool,
              norm_mix_pre, norm_mix_post, norm_ffn_pre, norm_ffn_post, w_in, w_out,
              attn_sinks, w_pool, pool_scale, w_gate, w_up, w_down, w_ple, w_ple_gate):
    yp = x_prompt
    ys = x_sample
    kp_l, vp_l, sp_l, ks_l, vs_l, ss_l = [], [], [], [], [], []
    pool_zero = jnp.zeros((x_prompt.shape[0], POOL_STATE, POOL_WIDTH), x_prompt.dtype)
    for i in range(DEPTH):
        lw = (norm_mix_pre[i], norm_mix_post[i], norm_ffn_pre[i], norm_ffn_post[i],
              w_in[i], w_out[i], w_pool[i], pool_scale[i], w_gate[i], w_up[i], w_down[i],
              w_ple[i], w_ple_gate[i])
        sinks = attn_sinks[i].reshape(N_KV_HEADS, GROUP)
        yp, kp, vp, sp = _layer(yp, p_prompt[i], functools.partial(_swa_prompt, sinks=sinks),
                                pool_zero, 0, *lw)
        attn_s = functools.partial(_swa_sample, k_buf=cache_k[i], v_buf=cache_v[i], sinks=sinks)
        ys, kss, vss, sss = _layer(ys, p_sample[i], attn_s, state_pool[i], PAST_LEN, *lw)
        kp_l.append(kp); vp_l.append(vp); sp_l.append(sp)
        ks_l.append(kss); vs_l.append(vss); ss_l.append(sss)
    k_prompt = jnp.stack(kp_l)
    v_prompt = jnp.stack(vp_l)
    pool_prompt = jnp.stack(sp_l)
    k_sample = jnp.stack(ks_l)
    v_sample = jnp.stack(vs_l)
    pool_sample = jnp.stack(ss_l)
    return (yp, ys, k_prompt, v_prompt, pool_prompt, k_sample, v_sample, pool_sample)
```

```python
import numpy as np
import ml_dtypes
import concourse.bass as bass
import concourse.mybir as mybir
from concourse.bass_utils import run_bass_kernel_spmd

F32 = mybir.dt.float32
BF16 = mybir.dt.bfloat16
AF = mybir.ActivationFunctionType
ALU = mybir.AluOpType

D = 1024
NCH = 8
DFF = 2816
NFC = 22
SEQ = 2048
GT = 1024
NS = 64
TT = GT + NS
NB = 16
DEPTH = 4
NEG = -30000.0
EPS = 1e-6
WINS = (2, 4, 8, 16)
SB_BASE = 16512
SB_END = 229344
import os
SKIP = set(os.environ.get('SKIP', '').split(','))


class _Stop(Exception):
    pass


class Op:
    __slots__ = ("eng", "fn", "li", "dma", "dval", "sig", "semval", "waits", "clock")


class Prog:
    ENGS = ("pe", "act", "dve", "pool", "sp")

    def __init__(self):
        self.q = {e: [] for e in self.ENGS}
        self.res = {}
        self.known = {e: {} for e in self.ENGS}
        self.dma_cnt = {}
        self.dma_alias = set()
        self.bar = {e: None for e in self.ENGS}
        self.names = {}

    def add(self, eng, fn, reads=(), writes=(), dma=None, alias=False):
        op = Op()
        op.eng = eng
        op.fn = fn
        op.dma = dma
        op.sig = False
        op.semval = 0
        deps = []
        for r in reads:
            st = self.res.get(r)
            if st is None:
                st = self.res[r] = [[], []]
            deps.extend(st[0])
            if r.__class__ is tuple and r[0] == "ps":
                deps.extend(p for p in st[1] if p.eng != eng)
            st[1].append(op)
        for w in writes:
            st = self.res.get(w)
            if st is None:
                st = self.res[w] = [[], []]
            deps.extend(st[0])
            deps.extend(st[1])
            st[0] = [op]
            st[1] = []
        kn = self.known[eng]
        need = {}
        b = self.bar[eng]
        if b is not None:
            self.bar[eng] = None
            bops, bd = b
            deps.extend(bops)
            for g, v in bd.items():
                key = ("d", g)
                if kn.get(key, 0) < v:
                    need[key] = (v, None)
        for p in deps:
            if p is op:
                continue
            if p.dma is not None:
                key = ("d", p.dma)
                val = self.dma_cnt[p.dma] * 16
            else:
                if p.eng == "pe" and eng == "pe":
                    continue
                key = p.eng
                val = p.li + 1
            if kn.get(key, 0) >= val:
                continue
            cur = need.get(key)
            if cur is None or cur[0] < val:
                need[key] = (val, p)
        op.waits = []
        for key, (val, p) in need.items():
            if kn.get(key, 0) >= val:
                continue
            op.waits.append((key, val, p))
            kn[key] = val
            if p is not None and p.dma is None:
                p.sig = True
                for k2, v2 in p.clock.items():
                    if kn.get(k2, 0) < v2:
                        kn[k2] = v2
        op.li = len(self.q[eng])
        self.q[eng].append(op)
        if dma is not None:
            c = self.dma_cnt.get(dma, 0) + 1
            self.dma_cnt[dma] = c
            op.dval = c * 16
            if alias:
                self.dma_alias.add(dma)
            op.clock = None
        else:
            ck = dict(kn)
            ck[eng] = op.li + 1
            op.clock = ck
        return op

    def barrier(self):
        bops = [self.q[e][-1] for e in ("pe", "act", "dve") if self.q[e]]
        pc = [o for o in self.q["pool"] if o.dma is None]
        if pc:
            bops.append(pc[-1])
        bd = {g: self.dma_cnt[g] * 16 for g in self.dma_alias}
        for e in self.ENGS:
            self.bar[e] = (bops, bd)


def build_program(NL=DEPTH, NG=2, stop_arg=None, stop_gi=0):
    nc = bass.Bass("TRN2", target_bir_lowering=False)
    P = Prog()

    def din(name, shape, dt=F32):
        return nc.dram_tensor(name, list(shape), dt, kind="ExternalInput").ap()

    def dout(name, shape):
        return nc.dram_tensor(name, list(shape), F32, kind="ExternalOutput").ap()

    xp_d = din("xp", [SEQ, D])
    xs_d = din("xs", [NS, D])
    pp_d = din("pp", [DEPTH, SEQ, 256])
    ps_d = din("psm", [DEPTH, NS, 256])
    ck_d = din("ck", [DEPTH, NB, 128, 128])
    cv_d = din("cv", [DEPTH, NB, 128, 128])
    sp_d = din("spool", [DEPTH, NB, 15, 512])
    nrm_d = [din(n, [DEPTH, D]) for n in ("n_mix_pre", "n_mix_post", "n_ffn_pre", "n_ffn_post")]
    w_in_d = din("w_in", [DEPTH, D, 1280])
    w_out_d = din("w_out", [DEPTH, D, D])
    sinks_d = din("sinks", [1, DEPTH * 8])
    w_pool_d = din("w_pool", [DEPTH, 4, 128, 128])
    pscale_d = din("pool_scale", [DEPTH, 512])
    w_gate_d = din("w_gate", [DEPTH, D, DFF])
    w_up_d = din("w_up", [DEPTH, D, DFF])
    w_down_d = din("w_down", [DEPTH, DFF, D])
    w_ple_d = din("w_ple", [DEPTH, 256, D])
    w_pg_d = din("w_ple_gate", [DEPTH, D, D])
    c_identb = din("c_identb", [128, 128], BF16)
    c_identf = din("c_identf", [128, 128])
    c_onesb = din("c_onesb", [128, 128], BF16)
    c_meanb = din("c_meanb", [128, 128], BF16)
    c_maskp2 = din("c_maskp2", [128, 512], BF16)
    c_maskd = din("c_maskd", [128, 512], BF16)
    c_masko = din("c_masko", [128, 512], BF16)
    c_masksc = din("c_masksc", [128, 512], BF16)
    c_masksn = din("c_masksn", [128, 256], BF16)
    c_invcnt = din("c_invcnt", [128, 64])

    yp_d = dout("y_prompt", [SEQ, D])
    ys_d = dout("y_sample", [NS, D])
    kp_d = dout("k_prompt", [DEPTH, 128, 128])
    vp_d = dout("v_prompt", [DEPTH, 128, 128])
    pop_d = dout("pool_prompt", [DEPTH, 15, 512])
    ks_d = dout("k_sample", [DEPTH, NB, 128, 128])
    vs_d = dout("v_sample", [DEPTH, NB, 128, 128])
    pos_d = dout("pool_sample", [DEPTH, NB, 15, 512])

    cur = [SB_BASE]

    def alloc(name, shape, dt, at=None):
        nbytes = int(np.prod(shape[1:])) * (4 if dt == F32 else 2)
        nbytes = (nbytes + 31) // 32 * 32
        if at is None:
            off = cur[0]
            cur[0] += nbytes
        else:
            off = at
        assert off + nbytes <= SB_END, (name, off, nbytes)
        h_ = nc.alloc_sbuf_tensor_at(name, list(shape), dt, offset=off)
        P.names[name] = h_.name
        return h_, off + nbytes

    def falloc(name, shape, dt):
        return alloc(name, shape, dt)[0]

    x = falloc("x", [128, NCH, TT], F32)
    hn = falloc("hn", [128, NCH, TT], BF16)
    am = hn
    identb = falloc("identb", [128, 128], BF16)
    identf = falloc("identf", [128, 128], F32)
    onesb = falloc("onesb", [128, 128], BF16)
    meanb = falloc("meanb", [128, 128], BF16)
    maskp2 = falloc("maskp2", [128, 512], BF16)
    maskd = falloc("maskd", [128, 512], BF16)
    masko = falloc("masko", [128, 512], BF16)
    masksc = falloc("masksc", [128, 512], BF16)
    masksn = falloc("masksn", [128, 256], BF16)
    invcnt = falloc("invcnt", [128, 4, 16], F32)
    epsT = falloc("epsT", [128, 1], F32)
    gains = falloc("gains", [128, NCH, 16], F32)
    pscale = falloc("pscale", [128, 4, 4], F32)
    sinkst = falloc("sinkst", [1, 32], F32)
    sinkex = falloc("sinkex", [1, 32], F32)
    onesf = falloc("onesf", [1, 128], F32)
    sinkrow_p = falloc("sinkrow_p", [1, 2, 512], BF16)
    sinkrow_s = falloc("sinkrow_s", [1, 512], BF16)
    kprev = falloc("kprev", [128, DEPTH * 2, 128], BF16)
    vprev = falloc("vprev", [128, DEPTH, 2, 128], BF16)
    uprev = falloc("uprev", [128, DEPTH * 4, 16], F32)
    sqtmp = [falloc("sqtmp%d" % i, [128, 512], BF16) for i in range(3)]
    sdt = [falloc("sd%d" % i, [128, 512], F32) for i in range(2)]
    rstd = [falloc("rstd%d" % i, [128, 512], F32) for i in range(2)]
    NSLOT = 4
    SLOT_EL = 4096
    wslot = [falloc("wslot%d" % i, [128, SLOT_EL], BF16) for i in range(NSLOT)]
    ptok = falloc("ptok", [128, 9, 256], BF16)
    pT = falloc("pT", [128, 2, TT], BF16)
    ubase = cur[0]

    cur[0] = ubase
    qT = falloc("qT", [128, 4, TT], BF16)
    kT = [falloc("kT%d" % i, [128, 128 + TT + 64], BF16) for i in range(2)]
    ut = falloc("ut", [128, 4, 16 + GT], F32)
    vd = falloc("vd", [128, 9, 2, 128], BF16)
    NPT = 4
    ptb = [falloc("ptb%d" % i, [128, 4, 256], BF16) for i in range(NPT)]
    alias0 = cur[0]
    ptmp = [falloc("ptmp%d" % i, [128, 16 + 512], F32) for i in range(2)]
    dl = falloc("dl", [128, 4, 512], BF16)
    alias1 = cur[0]
    cur[0] = alias0
    kvst = falloc("kvst", [128, 256], F32)
    ust = falloc("ust", [128, 512], F32)
    kvs = falloc("kvs", [64, 256], F32)
    usts = falloc("usts", [64, 512], F32)
    assert cur[0] <= alias1
    cur[0] = alias1
    recip = [falloc("recip%d" % i, [128, 512], F32) for i in range(2)]
    ckt = falloc("ckt", [128, NB, 128], BF16)
    ckT = [falloc("ckT%d" % i, [128, NB, 128], BF16) for i in range(2)]
    cvd = falloc("cvd", [128, NB, 2, 128], BF16)
    vnd = falloc("vnd", [128, 2, 128], BF16)
    uext = falloc("uext", [128, 4, NB, 19], F32)
    stp = falloc("stp", [120, 512], F32)
    stmp = [falloc("stmp%d" % i, [128, NB, 19], F32) for i in range(2)]
    dls = falloc("dls", [128, 4, NS], BF16)
    ptc = falloc("ptc", [128, 512], BF16)
    ptn = falloc("ptn", [128, 512], BF16)
    mix_end = cur[0]

    cur[0] = ubase
    mixg = falloc("mixg", [128, NCH, TT], F32)
    hT = falloc("hT", [128, NFC, TT], BF16)
    sgt = [falloc("sgt%d" % i, [128, 512], F32) for i in range(2)]
    ffn_end = cur[0]

    cur[0] = ubase
    xin = [falloc("xin%d" % i, [128, 4, D], F32) for i in range(2)]
    yst = [falloc("yst%d" % i, [128, D], F32) for i in range(2)]
    gstage = falloc("gstage", [16, D], F32)
    pstage = falloc("pstage", [4, 512], F32)
    io_end = cur[0]
    assert max(mix_end, ffn_end, io_end) <= SB_END, (mix_end, ffn_end, io_end)

    banks = [nc.alloc_psum_tensor("bank%d" % i, [128, 512], F32) for i in range(8)]
    bank_i = [0]

    bank_mod = [8]

    def nbank():
        b = bank_i[0] % bank_mod[0]
        bank_i[0] = (b + 1) % bank_mod[0]
        return b

    def MM(out, lhsT, rhs, start, stop, reads, bank):
        P.add("pe", lambda e: e.matmul(out, lhsT=lhsT, rhs=rhs, start=start, stop=stop),
              reads=reads, writes=[("ps", bank)])

    def TR(out, in_, ident, reads, bank):
        P.add("pe", lambda e: e.transpose(out, in_, ident), reads=reads, writes=[("ps", bank)])

    def ACT(out, in_, func, reads, writes, bias=None, scale=None):
        kw = {}
        if bias is not None:
            kw["bias"] = bias
        if scale is not None:
            kw["scale"] = scale
        P.add("act", lambda e: e.activation(out, in_, func, **kw), reads=reads, writes=writes)

    def DVE(fn, reads, writes):
        P.add("dve", fn, reads=reads, writes=writes)

    def VTT(out, in0, in1, op, reads, writes):
        P.add("dve", lambda e: e.tensor_tensor(out, in0, in1, op), reads=reads, writes=writes)

    def TS(out, in0, s1, op0, reads, writes):
        P.add("dve", lambda e: e.tensor_scalar(out, in0, s1, None, op0), reads=reads, writes=writes)

    def STT(out, in0, scalar, in1, op0, op1, reads, writes):
        P.add("dve", lambda e: e.scalar_tensor_tensor(out, in0, scalar, in1, op0, op1), reads=reads, writes=writes)

    def RECIP(out, in_, reads, writes):
        P.add("dve", lambda e: e.reciprocal(out, in_), reads=reads, writes=writes)

    def MEMSET(ap, val, writes):
        P.add("dve", lambda e: e.memset(ap, val), reads=[], writes=writes)

    evac_tog = [0]

    def COPY(out, in_, reads, writes, eng=None):
        if eng is None:
            eng = ("act", "dve")[evac_tog[0] & 1]
            evac_tog[0] += 1
        if eng == "act":
            ACT(out, in_, AF.Copy, reads, writes)
        else:
            DVE(lambda e: e.tensor_copy(out, in_), reads, writes)

    def DMA(eng, out, in_, reads, writes, group, alias=False):
        P.add(eng, lambda e: e.dma_start(out=out, in_=in_), reads=reads, writes=writes,
              dma=group, alias=alias)

    def cts_of(gi):
        c = [(0, 0, 512), (1, 512, 512)]
        if gi == 1:
            c.append((2, GT, NS))
        return c

    units = []

    def wview(slot, a, b):
        return wslot[slot][:, 0:a * b].rearrange("p (a b) -> p a b", a=a)

    def plan_units(l):
        us = []
        for i in range(5):
            us.append(("in%d" % i, w_in_d[l, :, i * 256:(i + 1) * 256].rearrange("(kc p) n -> p kc n", p=128), 8, 256))
        us.append(("pool", w_pool_d[l].rearrange("g c d -> c g d"), 4, 128))
        for i in range(4):
            us.append(("out%d" % i, w_out_d[l, :, i * 256:(i + 1) * 256].rearrange("(kc p) n -> p kc n", p=128), 8, 256))
        for i in range(11):
            us.append(("gate%d" % i, w_gate_d[l, :, i * 256:(i + 1) * 256].rearrange("(kc p) n -> p kc n", p=128), 8, 256))
            us.append(("up%d" % i, w_up_d[l, :, i * 256:(i + 1) * 256].rearrange("(kc p) n -> p kc n", p=128), 8, 256))
        for i in range(8):
            us.append(("down%d" % i, w_down_d[l, :, i * 128:(i + 1) * 128].rearrange("(kc p) n -> p kc n", p=128), NFC, 128))
        for i in range(4):
            us.append(("ple%d" % i, w_ple_d[l, :, i * 256:(i + 1) * 256].rearrange("(kc p) n -> p kc n", p=128), 2, 256))
            us.append(("pg%d" % i, w_pg_d[l, :, i * 256:(i + 1) * 256].rearrange("(kc p) n -> p kc n", p=128), 8, 256))
        return us

    for gi in range(NG):
        for l in range(NL):
            units.extend(plan_units(l))
    wstate = {"issued": 0, "cur": 0}

    def issue_weight():
        i = wstate["issued"]
        if i >= len(units):
            return
        wstate["issued"] = i + 1
        name, src, a, b = units[i]
        s = i % NSLOT
        DMA("pool", wview(s, a, b), src, [], [("ws", s)], "ws%d" % s)

    def next_unit(expect):
        i = wstate["cur"]
        wstate["cur"] = i + 1
        name, src, a, b = units[i]
        assert name == expect, (name, expect)
        s = i % NSLOT
        return s, wview(s, a, b)

    def release_unit():
        issue_weight()

    for _ in range(NSLOT):
        issue_weight()

    for t, d_, nm in ((identb, c_identb, "identb"), (identf, c_identf, "identf"), (onesb, c_onesb, "onesb"),
                      (meanb, c_meanb, "meanb"), (maskp2, c_maskp2, "maskp2"), (maskd, c_maskd, "maskd"),
                      (masko, c_masko, "masko"), (masksc, c_masksc, "masksc"), (masksn, c_masksn, "masksn")):
        DMA("sp", t[:], d_, [], [nm], "const")
    DMA("sp", invcnt[:].rearrange("p a b -> p (a b)"), c_invcnt, [], ["invcnt"], "const")
    DMA("sp", sinkst[:], sinks_d, [], ["sinkst"], "const")
    for n in range(4):
        DMA("sp", gstage[4 * n:4 * n + 4, :], nrm_d[n], [], ["gstage"], "gst", alias=True)
    DMA("sp", pstage[:], pscale_d, [], ["pstage"], "gst", alias=True)
    MEMSET(epsT[:], EPS, ["epsT"])
    MEMSET(onesf[:], 1.0, ["onesf"])
    for c in range(NCH):
        b = nbank()
        TR(banks[b][:, 0:16], gstage[0:16, c * 128:(c + 1) * 128], identf[0:16, 0:16], ["gstage", "identf"], b)
        COPY(gains[:, c, :], banks[b][:, 0:16], [("ps", b)], ["gains"])
    for g in range(4):
        b = nbank()
        TR(banks[b][:, 0:4], pstage[0:4, g * 128:(g + 1) * 128], identf[0:4, 0:4], ["pstage", "identf"], b)
        COPY(pscale[:, g, :], banks[b][:, 0:4], [("ps", b)], ["pscale"])
    ACT(sinkex[:], sinkst[:], AF.Exp, ["sinkst"], ["sinkex"])
    def gain_ap(n, l, c):
        return gains[:, c, 4 * n + l:4 * n + l + 1]

    sq_i = [0]

    def rstd_from_ms(cti, n):
        r = cti & 1
        mb = 5 + cti
        ACT(sdt[r][:, 0:n], banks[mb][:, 0:n], AF.Ln, [("ps", mb), "epsT"], [("sd", r)], bias=epsT[:, 0:1])
        ACT(rstd[r][:, 0:n], sdt[r][:, 0:n], AF.Exp, [("sd", r)], [("rstd", r)], scale=-0.5)
        return r

    def pre_norm(gi, l, nidx):
        for cti, c0, n in cts_of(gi):
            mb = 5 + cti
            for kc in range(NCH):
                si = sq_i[0] % 3
                sq_i[0] += 1
                ACT(sqtmp[si][:, 0:n], x[:, kc, c0:c0 + n], AF.Square, [("x", kc, cti)], [("sqt", si)])
                MM(banks[mb][:, 0:n], meanb[:, :], sqtmp[si][:, 0:n], kc == 0, kc == NCH - 1,
                   [("sqt", si), "meanb"], mb)
            r = rstd_from_ms(cti, n)
            for kc in range(NCH):
                STT(hn[:, kc, c0:c0 + n], x[:, kc, c0:c0 + n], gain_ap(nidx, l, kc), rstd[r][:, 0:n],
                    ALU.mult, ALU.mult, [("x", kc, cti), ("rstd", r), "gains"], [("hn", kc, cti)])

    for gi in range(NG):
      try:
        cts = cts_of(gi)
        tok0 = gi * GT
        stop = stop_arg if gi == stop_gi else None
        P.barrier()
        for cti, c0, n in cts:
            xb_ = cti & 1
            if cti < 2:
                DMA("sp", xin[xb_][:], xp_d[tok0 + c0:tok0 + c0 + 512, :].rearrange("(tb p) d -> p tb d", p=128),
                    [], [("xin", xb_)], "xin%d" % xb_, alias=True)
            else:
                DMA("sp", xin[xb_][0:NS, 0, :], xs_d, [], [("xin", xb_)], "xin%d" % xb_, alias=True)
            for kc in range(NCH):
                b = nbank()
                if cti < 2:
                    for tb in range(4):
                        TR(banks[b][:, tb * 128:(tb + 1) * 128], xin[xb_][:, tb, kc * 128:(kc + 1) * 128],
                           identf[:, :], [("xin", xb_), "identf"], b)
                else:
                    TR(banks[b][:, 0:NS], xin[xb_][0:NS, 0, kc * 128:(kc + 1) * 128], identf[0:NS, 0:NS],
                       [("xin", xb_), "identf"], b)
                COPY(x[:, kc, c0:c0 + n], banks[b][:, 0:n], [("ps", b)], [("x", kc, cti)])

        if stop == 'x':
            return nc, P
        for l in range(NL):
            last_layer = (l == NL - 1)
            P.barrier()
            for j in range(8):
                h_, g_ = j // 4, j % 4
                idx = l * 8 + j
                TS(sinkrow_p[0:1, h_, g_ * 128:(g_ + 1) * 128], onesf[0:1, 0:128],
                   sinkex[0:1, idx:idx + 1], ALU.mult, ["sinkex", "onesf"], ["sinkrow_p"])
                if gi == 1:
                    TS(sinkrow_s[0:1, j * 64:(j + 1) * 64], onesf[0:1, 0:NS],
                       sinkex[0:1, idx:idx + 1], ALU.mult, ["sinkex", "onesf"], ["sinkrow_s"])
            DMA("pool", ptok[:, 0:8, :], pp_d[l, tok0:tok0 + GT, :].rearrange("(tb p) d -> p tb d", p=128),
                [], ["ptok"], "ptok")
            if gi == 1:
                DMA("pool", ptok[0:NS, 8, :], ps_d[l], [], ["ptok"], "ptok")
            for cti, c0, n in cts:
                for k2 in range(2):
                    b = nbank()
                    if cti < 2:
                        for tb in range(4):
                            MM(banks[b][:, tb * 128:(tb + 1) * 128], ptok[:, cti * 4 + tb, k2 * 128:(k2 + 1) * 128],
                               identb[:, :], True, True, ["ptok", "identb"], b)
                    else:
                        MM(banks[b][:, 0:NS], ptok[0:NS, 8, k2 * 128:(k2 + 1) * 128], identb[0:NS, 0:NS],
                           True, True, ["ptok", "identb"], b)
                    COPY(pT[:, k2, c0:c0 + n], banks[b][:, 0:n], [("ps", b)], [("pT", k2, cti)])
            if stop == 'L0':
                return nc, P
            if gi == 1:
                if 'cache' not in SKIP:
                    DMA("pool", ckt[:], ck_d[l].rearrange("b k f -> k b f"), [], ["ckt"], "ckt", alias=True)
                    for dup in range(2):
                        for h_ in range(2):
                            DMA("pool", cvd[:, :, h_, dup * 64:(dup + 1) * 64],
                                cv_d[l, :, :, h_ * 64:(h_ + 1) * 64].rearrange("b k d -> k b d"), [], ["cvd"], "cvd", alias=True)
                if 'd2d' not in SKIP:
                    DMA("pool", ks_d[l, :, 0:124, :], ck_d[l, :, 4:128, :], [], [], "d2d")
                    DMA("pool", vs_d[l, :, 0:124, :], cv_d[l, :, 4:128, :], [], [], "d2d")
                    DMA("pool", pos_d[l, :, 0:11, :], sp_d[l, :, 4:15, :], [], [], "d2d")
                for sw in range(2):
                    COPY(kT[sw][:, 0:128], kprev[:, l * 2 + sw, :], ["kprev"], [("kT", sw, "m")])
                COPY(vd[:, 0, :, :], vprev[:, l, :, :], ["vprev"], [("vd", 0)])
                COPY(ut[:, :, 0:16], uprev[:, 4 * l:4 * l + 4, :], ["uprev"], [("u", "m")])
                for sw in range(2):
                    MEMSET(kT[sw][:, 128 + TT:128 + TT + 64], 0.0, [("kT", sw, "pad")])
                MEMSET(vnd[64:128, :, :], 0.0, ["vnd"])
            else:
                MEMSET(ut[:, :, 0:16], 0.0, [("u", "m")])

            if stop == 'L1':
                return nc, P
            pre_norm(gi, l, 0)
            if stop == 'A0':
                return nc, P
            for ui in range(5):
                s, wv = next_unit("in%d" % ui)
                wres = ("ws", s)
                for half in range(2):
                    oc = 2 * ui + half
                    if oc == 5:
                        continue
                    for cti, c0, n in cts:
                        if oc == 4 and cti == 2 and 'k2' in SKIP:
                            continue
                        b = nbank()
                        for kc in range(NCH):
                            MM(banks[b][:, 0:n], wv[:, kc, half * 128:(half + 1) * 128], hn[:, kc, c0:c0 + n],
                               kc == 0, kc == NCH - 1, [wres, ("hn", kc, cti)], b)
                        if oc < 4:
                            COPY(qT[:, oc, c0:c0 + n], banks[b][:, 0:n], [("ps", b)], [("q", oc, cti)])
                        elif oc == 4:
                            COPY(kT[0][:, 128 + c0:128 + c0 + n], banks[b][:, 0:n], [("ps", b)], [("kT", 0, cti)])
                            b2 = nbank()
                            for hh in range(2):
                                for kc in range(NCH):
                                    MM(banks[b2][64 * hh:64 * hh + 64, 0:n],
                                       wv[:, kc, 64 * (1 - hh):64 * (1 - hh) + 64], hn[:, kc, c0:c0 + n],
                                       kc == 0, kc == NCH - 1, [wres, ("hn", kc, cti)], b2)
                            COPY(kT[1][:, 128 + c0:128 + c0 + n], banks[b2][:, 0:n], [("ps", b2)], [("kT", 1, cti)])
                        else:
                            g = oc - 6
                            if cti < 2:
                                COPY(ut[:, g, 16 + c0:16 + c0 + n], banks[b][:, 0:n], [("ps", b)], [("u", g, cti)])
                            else:
                                COPY(uext[:, g, :, 15:19], banks[b][:, 0:NS].rearrange("p (b s) -> p b s", b=NB),
                                     [("ps", b)], [("uext", g)])
                if ui == 2:
                    for cti, c0, n in cts:
                        if cti < 2:
                            for tb in range(4):
                                blk = cti * 4 + tb
                                lastb = (gi == 1 and blk == 7 and 'lastb' not in SKIP)
                                b = nbank()
                                cA = 0 if lastb else 128
                                for kc in range(NCH):
                                    MM(banks[b][:, cA:256], hn[:, kc, c0 + tb * 128:c0 + (tb + 1) * 128],
                                       wv[:, kc, cA:256], kc == 0, kc == NCH - 1, [wres, ("hn", kc, cti)], b)
                                vsrc = banks[b][:, 128:256].rearrange("p (h d) -> p h d", h=2)
                                COPY(vd[:, 1 + blk, :, 0:64], vsrc, [("ps", b)], [("vd", 1 + blk)], eng="act")
                                COPY(vd[:, 1 + blk, :, 64:128], vsrc, [("ps", b)], [("vd", 1 + blk)], eng="dve")
                                if lastb:
                                    COPY(kvst[:, :], banks[b][:, 0:256], [("ps", b)], ["kvst"])
                                    if 'tokout' not in SKIP:
                                        DMA("sp", kp_d[l], kvst[:, 0:128], ["kvst"], [], "kvst", alias=True)
                                        DMA("sp", vp_d[l], kvst[:, 128:256], ["kvst"], [], "kvst", alias=True)
                        elif 'stok' not in SKIP:
                            b = nbank()
                            for kc in range(NCH):
                                MM(banks[b][0:NS, 0:256], hn[:, kc, c0:c0 + NS], wv[:, kc, 0:256],
                                   kc == 0, kc == NCH - 1, [wres, ("hn", kc, cti)], b)
                            vsrc = banks[b][0:NS, 128:256].rearrange("p (h d) -> p h d", h=2)
                            COPY(vnd[0:NS, :, 0:64], vsrc, [("ps", b)], ["vnd"], eng="act")
                            COPY(vnd[0:NS, :, 64:128], vsrc, [("ps", b)], ["vnd"], eng="dve")
                            COPY(kvs[:, :], banks[b][0:NS, 0:256], [("ps", b)], ["kvs"])
                            if 'tokout' not in SKIP:
                                DMA("sp", ks_d[l, :, 124:128, :], kvs[:, 0:128], ["kvs"], [], "kvs", alias=True)
                                DMA("sp", vs_d[l, :, 124:128, :], kvs[:, 128:256], ["kvs"], [], "kvs", alias=True)
                if ui >= 3 and gi == 1:
                    hcol = (ui - 3) * 256
                    b = nbank()
                    for kc in range(NCH):
                        MM(banks[b][:, 0:256], hn[:, kc, 896:1024], wv[:, kc, 0:256], kc == 0, kc == NCH - 1,
                           [wres, ("hn", kc, 1)], b)
                    COPY(ust[:, hcol:hcol + 256], banks[b][:, 0:256], [("ps", b)], ["ust"])
                    b = nbank()
                    for kc in range(NCH):
                        MM(banks[b][0:NS, 0:256], hn[:, kc, GT:GT + NS], wv[:, kc, 0:256], kc == 0, kc == NCH - 1,
                           [wres, ("hn", kc, 2)], b)
                    COPY(usts[:, hcol:hcol + 256], banks[b][0:NS, 0:256], [("ps", b)], ["usts"])
                    if ui == 4:
                        if 'tokout' not in SKIP:
                            DMA("sp", pop_d[l], ust[113:128, :], ["ust"], [], "ust", alias=True)
                            DMA("sp", pos_d[l, :, 11:15, :], usts[:, :], ["usts"], [], "usts", alias=True)
                release_unit()
                if stop == 'Au%d' % ui:
                    return nc, P

            if stop == 'A':
                return nc, P
            if gi == 0 and NG > 1:
                for sw in range(2):
                    COPY(kprev[:, l * 2 + sw, :], kT[sw][:, 128 + 896:128 + 1024], [("kT", sw, 1)], ["kprev"])
                COPY(vprev[:, l, :, :], vd[:, 8, :, :], [("vd", 8)], ["vprev"])
                COPY(uprev[:, 4 * l:4 * l + 4, :], ut[:, :, GT:GT + 16], [("u", g, 1) for g in range(4)], ["uprev"])

            if gi == 1:
                P.barrier()
            ps_, pwv = next_unit("pool")
            pwres = ("ws", ps_)

            def pool_prompt_gen():
                for cti, c0, n in cts:
                    if cti == 2:
                        continue
                    base = 16 + c0
                    for g in range(4):
                        src = lambda a, b_, g=g: ut[:, g, a:b_]
                        src_res = [("u", "m")] + [("u", g, cc) for cc in range(cti + 1)]
                        lo = 0
                        ti = 0
                        step = 1
                        for lev in range(g + 1):
                            nlo = lo - 0
                            rem = (WINS[g] - 1) - (2 * step - 1)
                            o0 = base - rem
                            dst = ptmp[ti]
                            off = 16 - rem
                            width = 512 + rem
                            VTT(dst[:, off:off + width], src(o0, o0 + width), src(o0 - step, o0 - step + width), ALU.add,
                               src_res, [("ptmp", ti)])
                            tsel = ti
                            src = lambda a, b_, dst=dst, base=base: dst[:, a - (base - 16):b_ - (base - 16)]
                            src_res = [("ptmp", tsel)]
                            ti ^= 1
                            step *= 2
                        tot = src(base, base + 512)
                        STT(dl[:, g, :], tot, 1.0 / WINS[g], ut[:, g, base:base + 512], ALU.mult, ALU.subtract,
                            src_res + [("u", g, cti)], [("dl", g)])
                        if gi == 0 and cti == 0:
                            t16 = src(base, base + 16)
                            other = ptmp[ti]
                            VTT(other[:, 0:16], t16, invcnt[:, g, :], ALU.mult, src_res + ["invcnt"], [("ptmp", ti)])
                            VTT(dl[:, g, 0:16], other[:, 0:16], ut[:, g, base:base + 16], ALU.subtract,
                               [("ptmp", ti), ("u", g, cti)], [("dl", g)])
                        b = nbank()
                        MM(banks[b][:, :], pwv[:, g, :], dl[:, g, :], True, True, [pwres, ("dl", g)], b)
                        ACT(am[:, 4 + g, c0:c0 + 512], banks[b][:, :], AF.Copy, [("ps", b), "pscale"],
                            [("hn", 4 + g, cti)], scale=pscale[:, g, l:l + 1])
                        yield

            pgen = pool_prompt_gen()
            def khead(h, j):
                bq = 64 * (j % 2)
                if bq == 64 * h:
                    return 0, bq
                return 1, bq

            pt_i = [0]
            ptmap = {}
            for i in range(-1 if gi == 1 else 0, 8):
                kc0 = 128 * (i + 1)
                qlo, qhi = max(i, 0), min(i + 1, 7)
                nq = 128 * (qhi - qlo + 1)
                kcti = 0 if i < 4 else 1
                for h in range(2):
                    pi = pt_i[0] % NPT
                    pt_i[0] += 1
                    ptmap[(i, h)] = pi
                    if nq == 256:
                        msk, mname = maskp2, "maskp2"
                    elif i < 0:
                        msk, mname = masko, "masko"
                    else:
                        msk, mname = maskd, "maskd"
                    ncol = 2 * nq
                    bks = (nbank(), nbank())
                    for bk in bks:
                        MM(banks[bk][:, 0:ncol], identb[:, :], msk[:, 0:ncol], True, False, ["identb", mname], bk)
                    for g in range(4):
                        e_, a_ = g % 2, g // 2
                        bk = bks[e_]
                        j = 4 * h + g
                        sw, bq = khead(h, j)
                        kres = ("kT", sw, "m") if i < 0 else ("kT", sw, kcti)
                        MM(banks[bk][:, a_ * nq:(a_ + 1) * nq], kT[sw][bq:bq + 64, kc0:kc0 + 128],
                           qT[bq:bq + 64, j // 2, qlo * 128:qlo * 128 + nq], False, a_ == 1,
                           [kres, ("q", j // 2, qlo // 4), ("q", j // 2, qhi // 4)], bk)
                    for e_, bk in enumerate(bks):
                        ACT(ptb[pi][:, :, :].rearrange("p (a e) q -> p a e q", e=2)[:, :, e_, 0:nq],
                            banks[bk][:, 0:ncol].rearrange("p (a q) -> p a q", a=2),
                            AF.Exp, [("ps", bk)], [("pt", pi)], scale=0.125)
                if i >= 0:
                    qb = i
                    qcti = qb // 4
                    for h in range(2):
                        srcs = []
                        if qb >= 1 or gi == 1:
                            pprev = ptmap[(qb - 1, h)]
                            nqprev = 128 if (qb - 1) < 0 else 256
                            srcs.append((pprev, qb, ptb[pprev][:, :, 128:256] if nqprev == 256 else ptb[pprev][:, :, 0:128]))
                        pcur = ptmap[(qb, h)]
                        srcs.append((pcur, qb + 1, ptb[pcur][:, :, 0:128]))
                        bo = nbank()
                        bd = nbank()
                        for si, (pi, vblk, rhs) in enumerate(srcs):
                            MM(banks[bo][:, :], vd[:, vblk, h, :], rhs, si == 0, si == len(srcs) - 1,
                               [("vd", vblk), ("pt", pi)], bo)
                        for si, (pi, vblk, rhs) in enumerate(srcs):
                            MM(banks[bd][:, :], onesb[:, :], rhs, si == 0, False, ["onesb", ("pt", pi)], bd)
                        MM(banks[bd][:, :], onesb[0:1, :], sinkrow_p[0:1, h, :], False, True,
                           ["onesb", "sinkrow_p"], bd)
                        r = h
                        ACT(recip[r][:, :], banks[bd][:, :], AF.Ln, [("ps", bd)], [("recip", r)])
                        ACT(recip[r][:, :], recip[r][:, :], AF.Exp, [("recip", r)], [("recip", r)], scale=-1.0)
                        for e_ in range(2):
                            o_ap = am[64 * e_:64 * e_ + 64, 2 * h:2 * h + 2, qb * 128:(qb + 1) * 128]
                            i0 = banks[bo][64 * e_:64 * e_ + 64, :].rearrange("p (a e q) -> p a e q", a=2, e=2)[:, :, e_, :]
                            i1 = recip[r][64 * e_:64 * e_ + 64, :].rearrange("p (a e q) -> p a e q", a=2, e=2)[:, :, e_, :]
                            VTT(o_ap, i0, i1, ALU.mult, [("ps", bo), ("recip", r)],
                               [("hn", 2 * h, qcti), ("hn", 2 * h + 1, qcti)])
                    next(pgen, None)

            for _ in pgen:
                pass
            if stop == 'B':
                return nc, P
            if gi == 1:
                cs = GT
                for b4 in range(NB // 4):
                    for sw in range(2):
                        b = nbank()
                        for bi in range(4):
                            bb = b4 * 4 + bi
                            if sw == 0:
                                MM(banks[b][:, bi * 128:(bi + 1) * 128], ckt[:, bb, :], identb[:, :], True, True,
                                   ["ckt", "identb"], b)
                            else:
                                for hh in range(2):
                                    MM(banks[b][64 * hh:64 * hh + 64, bi * 128:(bi + 1) * 128],
                                       ckt[:, bb, 64 * (1 - hh):64 * (1 - hh) + 64], identb[:, :], True, True,
                                       ["ckt", "identb"], b)
                        COPY(ckT[sw][:, b4 * 4:b4 * 4 + 4, :], banks[b][:, :].rearrange("p (b k) -> p b k", b=4),
                             [("ps", b)], [("ckT", sw)])
                if stop == 'B2a':
                    return nc, P
                bse = (nbank(), nbank())
                for bk in bse:
                    MM(banks[bk][:, 0:256], identb[:, :], masksc[:, 0:256], True, False, ["identb", "masksc"], bk)
                for bb in range(NB):
                    for j in range(8):
                        sw, bq = khead(j // 4, j)
                        c_ = (j // 2) * 64 + bb * 4
                        bk = bse[j % 2]
                        MM(banks[bk][:, c_:c_ + 4], ckT[sw][bq:bq + 64, bb, :],
                           qT[bq:bq + 64, j // 2, cs + 4 * bb:cs + 4 * bb + 4], False, (bb == NB - 1 and j >= 6),
                           [("ckT", sw), ("q", j // 2, 2)], bk)
                for e_, bk in enumerate(bse):
                    ACT(ptc[:, :].rearrange("p (a e c) -> p a e c", a=4, e=2)[:, :, e_, :],
                        banks[bk][:, 0:256].rearrange("p (a c) -> p a c", a=4), AF.Exp, [("ps", bk)], ["ptc"], scale=0.125)
                if stop == 'B2b':
                    return nc, P
                bne = (nbank(), nbank())
                for bk in bne:
                    MM(banks[bk][:, 0:256], identb[:, :], masksn[:, 0:256], True, False, ["identb", "masksn"], bk)
                for bb in range(NB):
                    for j in range(8):
                        sw, bq = khead(j // 4, j)
                        c_ = (j // 2) * 64 + bb * 4
                        bk = bne[j % 2]
                        MM(banks[bk][:, c_:c_ + 4], kT[sw][bq:bq + 64, 128 + cs:128 + cs + 128],
                           qT[bq:bq + 64, j // 2, cs + 4 * bb:cs + 4 * bb + 4], False, (bb == NB - 1 and j >= 6),
                           [("kT", sw, 2), ("kT", sw, "pad"), ("q", j // 2, 2)], bk)
                for e_, bk in enumerate(bne):
                    ACT(ptn[:, :].rearrange("p (a e c) -> p a e c", a=4, e=2)[:, :, e_, :],
                        banks[bk][:, 0:256].rearrange("p (a c) -> p a c", a=4), AF.Exp, [("ps", bk)], ["ptn"], scale=0.125)
                if stop == 'B2c':
                    return nc, P
                bo = nbank()
                for h in range(2):
                    MM(banks[bo][:, h * 256:(h + 1) * 256], vnd[:, h, :], ptn[:, h * 256:(h + 1) * 256], h == 0, False,
                       ["vnd", "ptn"], bo)
                for bb in range(NB):
                    for j in range(8):
                        c_ = j * 64 + bb * 4
                        MM(banks[bo][:, c_:c_ + 4], cvd[:, bb, j // 4, :], ptc[:, c_:c_ + 4], False,
                           (bb == NB - 1 and j == 7), ["cvd", "ptc"], bo)
                bd = nbank()
                MM(banks[bd][:, :], onesb[:, :], ptn[:, :], True, False, ["onesb", "ptn"], bd)
                MM(banks[bd][:, :], onesb[:, :], ptc[:, :], False, False, ["onesb", "ptc"], bd)
                MM(banks[bd][:, :], onesb[0:1, :], sinkrow_s[0:1, :], False, True, ["onesb", "sinkrow_s"], bd)
                ACT(recip[0][:, :], banks[bd][:, :], AF.Ln, [("ps", bd)], [("recip", 0)])
                ACT(recip[0][:, :], recip[0][:, :], AF.Exp, [("recip", 0)], [("recip", 0)], scale=-1.0)
                for j in range(8):
                    e_ = j % 2
                    VTT(am[64 * e_:64 * e_ + 64, j // 2, cs:cs + NS], banks[bo][64 * e_:64 * e_ + 64, j * 64:(j + 1) * 64],
                        recip[0][64 * e_:64 * e_ + 64, j * 64:(j + 1) * 64], ALU.mult,
                        [("ps", bo), ("recip", 0)], [("hn", j // 2, 2)])

            if stop == 'B2':
                return nc, P
            if gi == 1:
                for half in range(2):
                    DMA("sp", stp[:], sp_d[l, 8 * half:8 * half + 8].rearrange("b r f -> (b r) f"),
                        [], ["stp"], "stp", alias=True)
                    for g in range(4):
                        b = nbank()
                        TR(banks[b][:, 0:120], stp[0:120, g * 128:(g + 1) * 128], identf[0:120, 0:120],
                           ["stp", "identf"], b)
                        COPY(uext[:, g, 8 * half:8 * half + 8, 0:15],
                             banks[b][:, 0:120].rearrange("p (b r) -> p b r", b=8), [("ps", b)], [("uext", g)])
                b = nbank()
                for g in range(4):
                    src = lambda a, b_, g=g: uext[:, g, :, a:b_]
                    src_res = [("uext", g)]
                    ti = 0
                    step = 1
                    for lev in range(g + 1):
                        rem = (WINS[g] - 1) - (2 * step - 1)
                        o0 = 15 - rem
                        width = 4 + rem
                        dst = stmp[ti]
                        VTT(dst[:, :, o0:o0 + width], src(o0, o0 + width), src(o0 - step, o0 - step + width), ALU.add,
                           src_res, [("stmp", ti)])
                        src = lambda a, b_, dst=dst: dst[:, :, a:b_]
                        src_res = [("stmp", ti)]
                        ti ^= 1
                        step *= 2
                    tot = src(15, 19)
                    STT(dls[:, g, :].rearrange("p (b s) -> p b s", b=NB), tot, 1.0 / WINS[g], uext[:, g, :, 15:19],
                        ALU.mult, ALU.subtract, src_res + [("uext", g)], [("dls", g)])
                    MM(banks[b][:, g * NS:(g + 1) * NS], pwv[:, g, :], dls[:, g, :], True, True, [pwres, ("dls", g)], b)
                for g in range(4):
                    ACT(am[:, 4 + g, GT:GT + NS], banks[b][:, g * NS:(g + 1) * NS], AF.Copy, [("ps", b), "pscale"],
                        [("hn", 4 + g, 2)], scale=pscale[:, g, l:l + 1])
            release_unit()

            if stop == 'C':
                return nc, P
            P.barrier()
            def branch_out(unit_names, kchunks, src_tile, src_name, nidx, nhalf):
                pend = []
                bank_mod[0] = 5

                def flush(keep):
                    while len(pend) > keep:
                        pend.pop(0)()

                for ui, uname in enumerate(unit_names):
                    s_, wv = next_unit(uname)
                    wres = ("ws", s_)
                    for half in range(nhalf):
                        oc = ui * nhalf + half
                        for cti, c0, n in cts:
                            b = nbank()
                            for kc in range(kchunks):
                                MM(banks[b][:, 0:n], wv[:, kc, half * 128:(half + 1) * 128], src_tile[:, kc, c0:c0 + n],
                                   kc == 0, kc == kchunks - 1, [wres, (src_name, kc, cti)], b)
                            flush(1)
                            si = sq_i[0] % 3
                            sq_i[0] += 1
                            ACT(mixg[:, oc, c0:c0 + n], banks[b][:, 0:n], AF.Copy, [("ps", b), "gains"],
                                [("mg", oc, cti)], scale=gain_ap(nidx, l, oc))
                            ACT(sqtmp[si][:, 0:n], banks[b][:, 0:n], AF.Square, [("ps", b)], [("sqt", si)])
                            pend.append(lambda oc=oc, cti=cti, n=n, si=si: MM(
                                banks[5 + cti][:, 0:n], meanb[:, :], sqtmp[si][:, 0:n], oc == 0, oc == NCH - 1,
                                [("sqt", si), "meanb"], 5 + cti))
                    release_unit()
                    if stop == 'D0':
                        raise _Stop()
                if stop == 'D1':
                    raise _Stop()
                flush(0)
                bank_mod[0] = 8
                if stop == 'D2':
                    raise _Stop()
                for cti, c0, n in cts:
                    r = rstd_from_ms(cti, n)
                    rb = rstd[r][:, 0:n].unsqueeze(1).broadcast_to([128, 4, n])
                    for hf in range(2):
                        ocs = range(4 * hf, 4 * hf + 4)
                        VTT(mixg[:, 4 * hf:4 * hf + 4, c0:c0 + n], mixg[:, 4 * hf:4 * hf + 4, c0:c0 + n], rb, ALU.mult,
                           [("mg", oc, cti) for oc in ocs] + [("rstd", r)], [("mg", oc, cti) for oc in ocs])
                    for hf in range(2):
                        ocs = range(4 * hf, 4 * hf + 4)
                        xo = x[:, 4 * hf:4 * hf + 4, c0:c0 + n]
                        mo = mixg[:, 4 * hf:4 * hf + 4, c0:c0 + n]
                        P.add("pool", lambda e, xo=xo, mo=mo: e.tensor_tensor(xo, xo, mo, ALU.add),
                              reads=[("mg", oc, cti) for oc in ocs] + [("x", oc, cti) for oc in ocs],
                              writes=[("x", oc, cti) for oc in ocs])

            branch_out(["out%d" % i for i in range(4)], NCH, am, "hn", 1, 2)

            if stop == 'D':
                return nc, P
            pre_norm(gi, l, 2)
            for i in range(11):
                sg_, wg = next_unit("gate%d" % i)
                su_, wu = next_unit("up%d" % i)
                for half in range(2):
                    c = 2 * i + half
                    for cti, c0, n in cts:
                        bg = nbank()
                        for kc in range(NCH):
                            MM(banks[bg][:, 0:n], wg[:, kc, half * 128:(half + 1) * 128], hn[:, kc, c0:c0 + n],
                               kc == 0, kc == NCH - 1, [("ws", sg_), ("hn", kc, cti)], bg)
                        bu = nbank()
                        for kc in range(NCH):
                            MM(banks[bu][:, 0:n], wu[:, kc, half * 128:(half + 1) * 128], hn[:, kc, c0:c0 + n],
                               kc == 0, kc == NCH - 1, [("ws", su_), ("hn", kc, cti)], bu)
                        r = (c * 3 + cti) & 1
                        ACT(sgt[r][:, 0:n], banks[bg][:, 0:n], AF.Silu, [("ps", bg)], [("sg", r)])
                        VTT(hT[:, c, c0:c0 + n], sgt[r][:, 0:n], banks[bu][:, 0:n], ALU.mult,
                           [("sg", r), ("ps", bu)], [("h", c, cti)])
                release_unit()
                release_unit()
            branch_out(["down%d" % i for i in range(8)], NFC, hT, "h", 3, 1)

            if stop == 'E':
                return nc, P
            for cti, c0, n in cts:
                for half in range(2):
                    ACT(hn[:, 4 * half:4 * half + 4, c0:c0 + n], x[:, 4 * half:4 * half + 4, c0:c0 + n], AF.Copy,
                        [("x", kc, cti) for kc in range(4 * half, 4 * half + 4)],
                        [("hn", kc, cti) for kc in range(4 * half, 4 * half + 4)])
            for ui in range(4):
                se_, we = next_unit("ple%d" % ui)
                s, wv = next_unit("pg%d" % ui)
                for half in range(2):
                    oc = 2 * ui + half
                    for cti, c0, n in cts:
                        bg = nbank()
                        for kc in range(NCH):
                            MM(banks[bg][:, 0:n], wv[:, kc, half * 128:(half + 1) * 128], hn[:, kc, c0:c0 + n],
                               kc == 0, kc == NCH - 1, [("ws", s), ("hn", kc, cti)], bg)
                        be = nbank()
                        for k2 in range(2):
                            MM(banks[be][:, 0:n], we[:, k2, half * 128:(half + 1) * 128], pT[:, k2, c0:c0 + n],
                               k2 == 0, k2 == 1, [("ws", se_), ("pT", k2, cti)], be)
                        r = (oc + cti) & 1
                        ACT(sgt[r][:, 0:n], banks[bg][:, 0:n], AF.Sigmoid, [("ps", bg)], [("sg", r)])
                        VTT(sgt[r][:, 0:n], sgt[r][:, 0:n], banks[be][:, 0:n], ALU.mult,
                           [("sg", r), ("ps", be)], [("sg", r)])
                        VTT(x[:, oc, c0:c0 + n], x[:, oc, c0:c0 + n], sgt[r][:, 0:n], ALU.add,
                           [("sg", r), ("x", oc, cti)], [("x", oc, cti)])
                release_unit()
                release_unit()

        P.barrier()
        oi = 0
        for cti, c0, n in cts:
            ntb = 4 if cti < 2 else 1
            for tb in range(ntb):
                np_ = 128 if cti < 2 else NS
                yb = oi & 1
                oi += 1
                for hf in range(2):
                    b = nbank()
                    for k4 in range(4):
                        kc = hf * 4 + k4
                        TR(banks[b][0:np_, k4 * 128:(k4 + 1) * 128], x[:, kc, c0 + tb * 128:c0 + tb * 128 + np_],
                           identf[:, :], [("x", kc, cti), "identf"], b)
                    COPY(yst[yb][0:np_, hf * 512:(hf + 1) * 512], banks[b][0:np_, :], [("ps", b)], [("yst", yb, hf)])
                if cti < 2:
                    r0 = tok0 + c0 + tb * 128
                    DMA("sp", yp_d[r0:r0 + 128, :], yst[yb][:, :], [("yst", yb, 0), ("yst", yb, 1)], [], "yst%d" % yb, alias=True)
                else:
                    DMA("sp", ys_d, yst[yb][0:NS, :], [("yst", yb, 0), ("yst", yb, 1)], [], "yst%d" % yb, alias=True)

      except _Stop:
        return nc, P
    return nc, P


def finalize(nc, P):
    esem = {}
    for e in ("pe", "act", "dve", "pool"):
        esem[e] = nc.alloc_semaphore("sem_" + e)
        cnt = 0
        for op in P.q[e]:
            if op.sig:
                cnt += 1
                op.semval = cnt
    dsem = {g: nc.alloc_semaphore("dsem_" + g) for g in P.dma_cnt}

    def emit(name, e):
        for op in P.q[name]:
            for key, val, p in op.waits:
                if isinstance(key, tuple):
                    e.wait_ge(dsem[key[1]], val)
                else:
                    e.wait_ge(esem[key], p.semval)
            ins = op.fn(e)
            if op.dma is not None:
                ins.then_inc(dsem[op.dma], 16)
            elif op.sig:
                ins.then_inc(esem[name], 1)
        if name == "sp":
            for g, c in P.dma_cnt.items():
                e.wait_ge(dsem[g], c * 16)

    with nc.Block() as block:
        @block.tensor
        def _(e):
            emit("pe", e)

        @block.scalar
        def _(e):
            emit("act", e)

        @block.vector
        def _(e):
            emit("dve", e)

        @block.gpsimd
        def _(e):
            emit("pool", e)

        @block.sync
        def _(e):
            emit("sp", e)
    return nc


def host_consts():
    bf = ml_dtypes.bfloat16
    c = {}
    c["c_identb"] = np.eye(128, dtype=np.float32).astype(bf)
    c["c_identf"] = np.eye(128, dtype=np.float32)
    c["c_onesb"] = np.ones((128, 128), np.float32).astype(bf)
    c["c_meanb"] = np.full((128, 128), 1.0 / D, np.float32).astype(bf)
    k = np.arange(128)[:, None]
    q2 = np.arange(256)[None, :]
    band = np.where((q2 >= k) & (q2 < k + 128), 0.0, NEG).astype(np.float32)
    c["c_maskp2"] = np.concatenate([band, band], axis=1).astype(bf)
    qd = np.arange(128)[None, :]
    diag = np.where(k <= qd, 0.0, NEG).astype(np.float32)
    offd = np.where(qd < k, 0.0, NEG).astype(np.float32)
    c["c_maskd"] = np.tile(diag, (1, 4)).astype(bf)
    c["c_masko"] = np.tile(offd, (1, 4)).astype(bf)
    s_ = np.tile(np.arange(4), NB * 8)[None, :]
    c["c_masksc"] = np.where(k > s_, 0.0, NEG).astype(np.float32).astype(bf)
    bq = np.tile(np.repeat(np.arange(NB), 4), 8)[None, :]
    kb = (np.arange(NS) // 4)[:, None]
    ks = (np.arange(NS) % 4)[:, None]
    msn = np.full((128, 256), NEG, np.float32)
    msn[0:NS] = np.where((kb == bq) & (ks <= s_), 0.0, NEG)[:, 0:256]
    c["c_masksn"] = msn.astype(bf)
    inv = np.zeros((128, 4, 16), np.float32)
    for g, w in enumerate(WINS):
        for t in range(16):
            inv[:, g, t] = 1.0 / min(w, t + 1)
    c["c_invcnt"] = inv.reshape(128, 64)
    return c


_CACHE = {}


def kernel(x_prompt, x_sample, p_prompt, p_sample, cache_k, cache_v, state_pool,
           norm_mix_pre, norm_mix_post, norm_ffn_pre, norm_ffn_post, w_in, w_out,
           attn_sinks, w_pool, pool_scale, w_gate, w_up, w_down, w_ple, w_ple_gate):
    f = lambda a: np.ascontiguousarray(np.asarray(a, dtype=np.float32))
    if "nc" not in _CACHE:
        nc, P = build_program()
        finalize(nc, P)
        _CACHE["nc"] = nc
    nc = _CACHE["nc"]
    consts = host_consts()
    shared = {
        "n_mix_pre": f(norm_mix_pre), "n_mix_post": f(norm_mix_post), "n_ffn_pre": f(norm_ffn_pre),
        "n_ffn_post": f(norm_ffn_post), "w_in": f(w_in), "w_out": f(w_out),
        "sinks": f(attn_sinks).reshape(1, DEPTH * 8), "w_pool": f(w_pool), "pool_scale": f(pool_scale),
        "w_gate": f(w_gate), "w_up": f(w_up), "w_down": f(w_down), "w_ple": f(w_ple), "w_ple_gate": f(w_ple_gate),
    }
    shared.update(consts)
    xp, xs, pp, psm = f(x_prompt), f(x_sample), f(p_prompt), f(p_sample)
    ck, cv, spool = f(cache_k), f(cache_v), f(state_pool)
    in_maps = []
    for c in range(8):
        m = dict(shared)
        sl = slice(NB * c, NB * (c + 1))
        m["xp"] = xp[c]
        m["xs"] = xs[sl].reshape(NS, D)
        m["pp"] = np.ascontiguousarray(pp[:, c])
        m["psm"] = np.ascontiguousarray(psm[:, sl]).reshape(DEPTH, NS, 256)
        m["ck"] = np.ascontiguousarray(ck[:, sl]).reshape(DEPTH, NB, 128, 128)
        m["cv"] = np.ascontiguousarray(cv[:, sl]).reshape(DEPTH, NB, 128, 128)
        m["spool"] = np.ascontiguousarray(spool[:, sl])
        in_maps.append(m)
    res = run_bass_kernel_spmd(nc, in_maps, core_ids=list(range(8)))
    R = res.results
    y_prompt = np.stack([R[c]["y_prompt"] for c in range(8)], 0)
    y_sample = np.concatenate([R[c]["y_sample"].reshape(NB, 4, D) for c in range(8)], 0)
    k_prompt = np.stack([R[c]["k_prompt"].reshape(DEPTH, 128, 2, 64) for c in range(8)], 1)
    v_prompt = np.stack([R[c]["v_prompt"].reshape(DEPTH, 128, 2, 64) for c in range(8)], 1)
    pool_prompt = np.stack([R[c]["pool_prompt"] for c in range(8)], 1)
    k_sample = np.concatenate([R[c]["k_sample"].reshape(DEPTH, NB, 128, 2, 64) for c in range(8)], 1)
    v_sample = np.concatenate([R[c]["v_sample"].reshape(DEPTH, NB, 128, 2, 64) for c in range(8)], 1)
    pool_sample = np.concatenate([R[c]["pool_sample"] for c in range(8)], 1)
    return tuple(np.ascontiguousarray(a.astype(np.float32)) for a in
                 (y_prompt, y_sample, k_prompt, v_prompt, pool_prompt, k_sample, v_sample, pool_sample))
```

```python
import numpy as np
import ml_dtypes
import concourse.bass as bass
import concourse.mybir as mybir
from concourse.bass_utils import run_bass_kernel_spmd

F32 = mybir.dt.float32
BF16 = mybir.dt.bfloat16
AF = mybir.ActivationFunctionType
ALU = mybir.AluOpType

D = 1024
NCH = 8
DFF = 2816
NFC = 22
SEQ = 2048
GT = 1024
NS = 64
TT = GT + NS
NB = 16
DEPTH = 4
NEG = -30000.0
EPS = 1e-6
WINS = (2, 4, 8, 16)
SB_BASE = 16512
SB_END = 229344
import os
SKIP = set(os.environ.get('SKIP', '').split(','))


class _Stop(Exception):
    pass


class Op:
    __slots__ = ("eng", "fn", "li", "dma", "dval", "sig", "semval", "waits", "clock")


class Prog:
    ENGS = ("pe", "act", "dve", "pool", "sp")

    def __init__(self):
        self.q = {e: [] for e in self.ENGS}
        self.res = {}
        self.known = {e: {} for e in self.ENGS}
        self.dma_cnt = {}
        self.dma_alias = set()
        self.bar = {e: None for e in self.ENGS}
        self.names = {}

    def add(self, eng, fn, reads=(), writes=(), dma=None, alias=False):
        op = Op()
        op.eng = eng
        op.fn = fn
        op.dma = dma
        op.sig = False
        op.semval = 0
        deps = []
        for r in reads:
            st = self.res.get(r)
            if st is None:
                st = self.res[r] = [[], []]
            deps.extend(st[0])
            if r.__class__ is tuple and r[0] == "ps":
                deps.extend(p for p in st[1] if p.eng != eng)
            st[1].append(op)
        for w in writes:
            st = self.res.get(w)
            if st is None:
                st = self.res[w] = [[], []]
            deps.extend(st[0])
            deps.extend(st[1])
            st[0] = [op]
            st[1] = []
        kn = self.known[eng]
        need = {}
        b = self.bar[eng]
        if b is not None:
            self.bar[eng] = None
            bops, bd = b
            deps.extend(bops)
            for g, v in bd.items():
                key = ("d", g)
                if kn.get(key, 0) < v:
                    need[key] = (v, None)
        for p in deps:
            if p is op:
                continue
            if p.dma is not None:
                key = ("d", p.dma)
                val = self.dma_cnt[p.dma] * 16
            else:
                if p.eng == "pe" and eng == "pe":
                    continue
                key = p.eng
                val = p.li + 1
            if kn.get(key, 0) >= val:
                continue
            cur = need.get(key)
            if cur is None or cur[0] < val:
                need[key] = (val, p)
        op.waits = []
        for key, (val, p) in need.items():
            if kn.get(key, 0) >= val:
                continue
            op.waits.append((key, val, p))
            kn[key] = val
            if p is not None and p.dma is None:
                p.sig = True
                for k2, v2 in p.clock.items():
                    if kn.get(k2, 0) < v2:
                        kn[k2] = v2
        op.li = len(self.q[eng])
        self.q[eng].append(op)
        if dma is not None:
            c = self.dma_cnt.get(dma, 0) + 1
            self.dma_cnt[dma] = c
            op.dval = c * 16
            if alias:
                self.dma_alias.add(dma)
            op.clock = None
        else:
            ck = dict(kn)
            ck[eng] = op.li + 1
            op.clock = ck
        return op

    def barrier(self):
        bops = [self.q[e][-1] for e in ("pe", "act", "dve") if self.q[e]]
        bd = {g: self.dma_cnt[g] * 16 for g in self.dma_alias}
        for e in self.ENGS:
            self.bar[e] = (bops, bd)


def build_program(NL=DEPTH, NG=2, stop_arg=None, stop_gi=0):
    nc = bass.Bass("TRN2", target_bir_lowering=False)
    P = Prog()

    def din(name, shape, dt=F32):
        return nc.dram_tensor(name, list(shape), dt, kind="ExternalInput").ap()

    def dout(name, shape):
        return nc.dram_tensor(name, list(shape), F32, kind="ExternalOutput").ap()

    xp_d = din("xp", [SEQ, D])
    xs_d = din("xs", [NS, D])
    pp_d = din("pp", [DEPTH, SEQ, 256])
    ps_d = din("psm", [DEPTH, NS, 256])
    ck_d = din("ck", [DEPTH, NB, 128, 128])
    cv_d = din("cv", [DEPTH, NB, 128, 128])
    sp_d = din("spool", [DEPTH, NB, 15, 512])
    nrm_d = [din(n, [DEPTH, D]) for n in ("n_mix_pre", "n_mix_post", "n_ffn_pre", "n_ffn_post")]
    w_in_d = din("w_in", [DEPTH, D, 1280])
    w_out_d = din("w_out", [DEPTH, D, D])
    sinks_d = din("sinks", [1, DEPTH * 8])
    w_pool_d = din("w_pool", [DEPTH, 4, 128, 128])
    pscale_d = din("pool_scale", [DEPTH, 512])
    w_gate_d = din("w_gate", [DEPTH, D, DFF])
    w_up_d = din("w_up", [DEPTH, D, DFF])
    w_down_d = din("w_down", [DEPTH, DFF, D])
    w_ple_d = din("w_ple", [DEPTH, 256, D])
    w_pg_d = din("w_ple_gate", [DEPTH, D, D])
    c_identb = din("c_identb", [128, 128], BF16)
    c_identf = din("c_identf", [128, 128])
    c_onesb = din("c_onesb", [128, 128], BF16)
    c_meanb = din("c_meanb", [128, 128], BF16)
    c_maskp2 = din("c_maskp2", [128, 512], BF16)
    c_maskd = din("c_maskd", [128, 512], BF16)
    c_masko = din("c_masko", [128, 512], BF16)
    c_masksc = din("c_masksc", [128, 512], BF16)
    c_masksn = din("c_masksn", [128, 256], BF16)
    c_invcnt = din("c_invcnt", [128, 64])

    yp_d = dout("y_prompt", [SEQ, D])
    ys_d = dout("y_sample", [NS, D])
    kp_d = dout("k_prompt", [DEPTH, 128, 128])
    vp_d = dout("v_prompt", [DEPTH, 128, 128])
    pop_d = dout("pool_prompt", [DEPTH, 15, 512])
    ks_d = dout("k_sample", [DEPTH, NB, 128, 128])
    vs_d = dout("v_sample", [DEPTH, NB, 128, 128])
    pos_d = dout("pool_sample", [DEPTH, NB, 15, 512])

    cur = [SB_BASE]

    def alloc(name, shape, dt, at=None):
        nbytes = int(np.prod(shape[1:])) * (4 if dt == F32 else 2)
        nbytes = (nbytes + 31) // 32 * 32
        if at is None:
            off = cur[0]
            cur[0] += nbytes
        else:
            off = at
        assert off + nbytes <= SB_END, (name, off, nbytes)
        h_ = nc.alloc_sbuf_tensor_at(name, list(shape), dt, offset=off)
        P.names[name] = h_.name
        return h_, off + nbytes

    def falloc(name, shape, dt):
        return alloc(name, shape, dt)[0]

    x = falloc("x", [128, NCH, TT], F32)
    hn = falloc("hn", [128, NCH, TT], BF16)
    am = hn
    identb = falloc("identb", [128, 128], BF16)
    identf = falloc("identf", [128, 128], F32)
    onesb = falloc("onesb", [128, 128], BF16)
    meanb = falloc("meanb", [128, 128], BF16)
    maskp2 = falloc("maskp2", [128, 512], BF16)
    maskd = falloc("maskd", [128, 512], BF16)
    masko = falloc("masko", [128, 512], BF16)
    masksc = falloc("masksc", [128, 512], BF16)
    masksn = falloc("masksn", [128, 256], BF16)
    invcnt = falloc("invcnt", [128, 4, 16], F32)
    epsT = falloc("epsT", [128, 1], F32)
    gains = falloc("gains", [128, NCH, 16], F32)
    pscale = falloc("pscale", [128, 4, 4], F32)
    sinkst = falloc("sinkst", [1, 32], F32)
    sinkex = falloc("sinkex", [1, 32], F32)
    onesf = falloc("onesf", [1, 128], F32)
    sinkrow_p = falloc("sinkrow_p", [1, 2, 512], BF16)
    sinkrow_s = falloc("sinkrow_s", [1, 512], BF16)
    kprev = falloc("kprev", [128, DEPTH * 2, 128], BF16)
    vprev = falloc("vprev", [128, DEPTH, 2, 128], BF16)
    uprev = falloc("uprev", [128, DEPTH * 4, 16], F32)
    sqtmp = [falloc("sqtmp%d" % i, [128, 512], BF16) for i in range(3)]
    sdt = [falloc("sd%d" % i, [128, 512], F32) for i in range(2)]
    rstd = [falloc("rstd%d" % i, [128, 512], F32) for i in range(2)]
    NSLOT = 4
    SLOT_EL = 4096
    wslot = [falloc("wslot%d" % i, [128, SLOT_EL], BF16) for i in range(NSLOT)]
    ptok = falloc("ptok", [128, 9, 256], BF16)
    pT = falloc("pT", [128, 2, TT], BF16)
    ubase = cur[0]

    cur[0] = ubase
    qT = falloc("qT", [128, 4, TT], BF16)
    kT = [falloc("kT%d" % i, [128, 128 + TT + 64], BF16) for i in range(2)]
    ut = falloc("ut", [128, 4, 16 + GT], F32)
    vd = falloc("vd", [128, 9, 2, 128], BF16)
    NPT = 4
    ptb = [falloc("ptb%d" % i, [128, 4, 256], BF16) for i in range(NPT)]
    alias0 = cur[0]
    ptmp = [falloc("ptmp%d" % i, [128, 16 + 512], F32) for i in range(2)]
    dl = falloc("dl", [128, 4, 512], BF16)
    alias1 = cur[0]
    cur[0] = alias0
    kvst = falloc("kvst", [128, 256], F32)
    ust = falloc("ust", [128, 512], F32)
    kvs = falloc("kvs", [64, 256], F32)
    usts = falloc("usts", [64, 512], F32)
    assert cur[0] <= alias1
    cur[0] = alias1
    recip = [falloc("recip%d" % i, [128, 512], F32) for i in range(2)]
    ckt = falloc("ckt", [128, NB, 128], BF16)
    ckT = [falloc("ckT%d" % i, [128, NB, 128], BF16) for i in range(2)]
    cvd = falloc("cvd", [128, NB, 2, 128], BF16)
    vnd = falloc("vnd", [128, 2, 128], BF16)
    uext = falloc("uext", [128, 4, NB, 19], F32)
    stp = falloc("stp", [120, 512], F32)
    stmp = [falloc("stmp%d" % i, [128, NB, 19], F32) for i in range(2)]
    dls = falloc("dls", [128, 4, NS], BF16)
    ptc = falloc("ptc", [128, 512], BF16)
    ptn = falloc("ptn", [128, 512], BF16)
    mix_end = cur[0]

    cur[0] = ubase
    mixg = falloc("mixg", [128, NCH, TT], F32)
    hT = falloc("hT", [128, NFC, TT], BF16)
    sgt = [falloc("sgt%d" % i, [128, 512], F32) for i in range(2)]
    ffn_end = cur[0]

    cur[0] = ubase
    xin = [falloc("xin%d" % i, [128, 4, D], F32) for i in range(2)]
    yst = [falloc("yst%d" % i, [128, D], F32) for i in range(2)]
    gstage = falloc("gstage", [16, D], F32)
    pstage = falloc("pstage", [4, 512], F32)
    io_end = cur[0]
    assert max(mix_end, ffn_end, io_end) <= SB_END, (mix_end, ffn_end, io_end)

    banks = [nc.alloc_psum_tensor("bank%d" % i, [128, 512], F32) for i in range(8)]
    bank_i = [0]

    bank_mod = [8]

    def nbank():
        b = bank_i[0] % bank_mod[0]
        bank_i[0] = (b + 1) % bank_mod[0]
        return b

    def MM(out, lhsT, rhs, start, stop, reads, bank):
        P.add("pe", lambda e: e.matmul(out, lhsT=lhsT, rhs=rhs, start=start, stop=stop),
              reads=reads, writes=[("ps", bank)])

    def TR(out, in_, ident, reads, bank):
        P.add("pe", lambda e: e.transpose(out, in_, ident), reads=reads, writes=[("ps", bank)])

    def ACT(out, in_, func, reads, writes, bias=None, scale=None):
        kw = {}
        if bias is not None:
            kw["bias"] = bias
        if scale is not None:
            kw["scale"] = scale
        P.add("act", lambda e: e.activation(out, in_, func, **kw), reads=reads, writes=writes)

    def DVE(fn, reads, writes):
        P.add("dve", fn, reads=reads, writes=writes)

    def VTT(out, in0, in1, op, reads, writes):
        P.add("dve", lambda e: e.tensor_tensor(out, in0, in1, op), reads=reads, writes=writes)

    def TS(out, in0, s1, op0, reads, writes):
        P.add("dve", lambda e: e.tensor_scalar(out, in0, s1, None, op0), reads=reads, writes=writes)

    def STT(out, in0, scalar, in1, op0, op1, reads, writes):
        P.add("dve", lambda e: e.scalar_tensor_tensor(out, in0, scalar, in1, op0, op1), reads=reads, writes=writes)

    def RECIP(out, in_, reads, writes):
        P.add("dve", lambda e: e.reciprocal(out, in_), reads=reads, writes=writes)

    def MEMSET(ap, val, writes):
        P.add("dve", lambda e: e.memset(ap, val), reads=[], writes=writes)

    evac_tog = [0]

    def COPY(out, in_, reads, writes, eng=None):
        if eng is None:
            eng = ("act", "dve")[evac_tog[0] & 1]
            evac_tog[0] += 1
        if eng == "act":
            ACT(out, in_, AF.Copy, reads, writes)
        else:
            DVE(lambda e: e.tensor_copy(out, in_), reads, writes)

    def DMA(eng, out, in_, reads, writes, group, alias=False):
        P.add(eng, lambda e: e.dma_start(out=out, in_=in_), reads=reads, writes=writes,
              dma=group, alias=alias)

    def cts_of(gi):
        c = [(0, 0, 512), (1, 512, 512)]
        if gi == 1:
            c.append((2, GT, NS))
        return c

    units = []

    def wview(slot, a, b):
        return wslot[slot][:, 0:a * b].rearrange("p (a b) -> p a b", a=a)

    def plan_units(l):
        us = []
        for i in range(5):
            us.append(("in%d" % i, w_in_d[l, :, i * 256:(i + 1) * 256].rearrange("(kc p) n -> p kc n", p=128), 8, 256))
        us.append(("pool", w_pool_d[l].rearrange("g c d -> c g d"), 4, 128))
        for i in range(4):
            us.append(("out%d" % i, w_out_d[l, :, i * 256:(i + 1) * 256].rearrange("(kc p) n -> p kc n", p=128), 8, 256))
        for i in range(11):
            us.append(("gate%d" % i, w_gate_d[l, :, i * 256:(i + 1) * 256].rearrange("(kc p) n -> p kc n", p=128), 8, 256))
            us.append(("up%d" % i, w_up_d[l, :, i * 256:(i + 1) * 256].rearrange("(kc p) n -> p kc n", p=128), 8, 256))
        for i in range(8):
            us.append(("down%d" % i, w_down_d[l, :, i * 128:(i + 1) * 128].rearrange("(kc p) n -> p kc n", p=128), NFC, 128))
        for i in range(4):
            us.append(("ple%d" % i, w_ple_d[l, :, i * 256:(i + 1) * 256].rearrange("(kc p) n -> p kc n", p=128), 2, 256))
            us.append(("pg%d" % i, w_pg_d[l, :, i * 256:(i + 1) * 256].rearrange("(kc p) n -> p kc n", p=128), 8, 256))
        return us

    for gi in range(NG):
        for l in range(NL):
            units.extend(plan_units(l))
    wstate = {"issued": 0, "cur": 0}

    def issue_weight():
        i = wstate["issued"]
        if i >= len(units):
            return
        wstate["issued"] = i + 1
        name, src, a, b = units[i]
        s = i % NSLOT
        DMA("pool", wview(s, a, b), src, [], [("ws", s)], "ws%d" % s)

    def next_unit(expect):
        i = wstate["cur"]
        wstate["cur"] = i + 1
        name, src, a, b = units[i]
        assert name == expect, (name, expect)
        s = i % NSLOT
        return s, wview(s, a, b)

    def release_unit():
        issue_weight()

    for _ in range(NSLOT):
        issue_weight()

    for t, d_, nm in ((identb, c_identb, "identb"), (identf, c_identf, "identf"), (onesb, c_onesb, "onesb"),
                      (meanb, c_meanb, "meanb"), (maskp2, c_maskp2, "maskp2"), (maskd, c_maskd, "maskd"),
                      (masko, c_masko, "masko"), (masksc, c_masksc, "masksc"), (masksn, c_masksn, "masksn")):
        DMA("sp", t[:], d_, [], [nm], "const")
    DMA("sp", invcnt[:].rearrange("p a b -> p (a b)"), c_invcnt, [], ["invcnt"], "const")
    DMA("sp", sinkst[:], sinks_d, [], ["sinkst"], "const")
    for n in range(4):
        DMA("sp", gstage[4 * n:4 * n + 4, :], nrm_d[n], [], ["gstage"], "gst", alias=True)
    DMA("sp", pstage[:], pscale_d, [], ["pstage"], "gst", alias=True)
    MEMSET(epsT[:], EPS, ["epsT"])
    MEMSET(onesf[:], 1.0, ["onesf"])
    for c in range(NCH):
        b = nbank()
        TR(banks[b][:, 0:16], gstage[0:16, c * 128:(c + 1) * 128], identf[0:16, 0:16], ["gstage", "identf"], b)
        COPY(gains[:, c, :], banks[b][:, 0:16], [("ps", b)], ["gains"])
    for g in range(4):
        b = nbank()
        TR(banks[b][:, 0:4], pstage[0:4, g * 128:(g + 1) * 128], identf[0:4, 0:4], ["pstage", "identf"], b)
        COPY(pscale[:, g, :], banks[b][:, 0:4], [("ps", b)], ["pscale"])
    ACT(sinkex[:], sinkst[:], AF.Exp, ["sinkst"], ["sinkex"])
    def issue_ptok(g_, l_):
        t0_ = g_ * GT
        DMA("pool", ptok[:, 0:8, :], pp_d[l_, t0_:t0_ + GT, :].rearrange("(tb p) d -> p tb d", p=128),
            [], ["ptok"], "ptok")
        if g_ == 1:
            DMA("pool", ptok[0:NS, 8, :], ps_d[l_], [], ["ptok"], "ptok")

    def gain_ap(n, l, c):
        return gains[:, c, 4 * n + l:4 * n + l + 1]

    sq_i = [0]

    def rstd_from_ms(cti, n):
        r = cti & 1
        mb = 5 + cti
        ACT(sdt[r][:, 0:n], banks[mb][:, 0:n], AF.Ln, [("ps", mb), "epsT"], [("sd", r)], bias=epsT[:, 0:1])
        ACT(rstd[r][:, 0:n], sdt[r][:, 0:n], AF.Exp, [("sd", r)], [("rstd", r)], scale=-0.5)
        return r

    def pre_norm(gi, l, nidx):
        for cti, c0, n in cts_of(gi):
            mb = 5 + cti
            for kc in range(NCH):
                si = sq_i[0] % 3
                sq_i[0] += 1
                ACT(sqtmp[si][:, 0:n], x[:, kc, c0:c0 + n], AF.Square, [("x", kc, cti)], [("sqt", si)])
                MM(banks[mb][:, 0:n], meanb[:, :], sqtmp[si][:, 0:n], kc == 0, kc == NCH - 1,
                   [("sqt", si), "meanb"], mb)
            r = rstd_from_ms(cti, n)
            for kc in range(NCH):
                STT(hn[:, kc, c0:c0 + n], x[:, kc, c0:c0 + n], gain_ap(nidx, l, kc), rstd[r][:, 0:n],
                    ALU.mult, ALU.mult, [("x", kc, cti), ("rstd", r), "gains"], [("hn", kc, cti)])

    for gi in range(NG):
      try:
        cts = cts_of(gi)
        tok0 = gi * GT
        stop = stop_arg if gi == stop_gi else None
        if gi == 0:
            P.barrier()
        for cti, c0, n in cts:
            xb_ = cti & 1
            if cti < 2:
                DMA("sp", xin[xb_][:], xp_d[tok0 + c0:tok0 + c0 + 512, :].rearrange("(tb p) d -> p tb d", p=128),
                    [], [("xin", xb_)], "xin%d" % xb_, alias=True)
            else:
                DMA("sp", xin[xb_][0:NS, 0, :], xs_d, [], [("xin", xb_)], "xin%d" % xb_, alias=True)
            for kc in range(NCH):
                b = nbank()
                if cti < 2:
                    for tb in range(4):
                        TR(banks[b][:, tb * 128:(tb + 1) * 128], xin[xb_][:, tb, kc * 128:(kc + 1) * 128],
                           identf[:, :], [("xin", xb_), "identf"], b)
                else:
                    TR(banks[b][:, 0:NS], xin[xb_][0:NS, 0, kc * 128:(kc + 1) * 128], identf[0:NS, 0:NS],
                       [("xin", xb_), "identf"], b)
                COPY(x[:, kc, c0:c0 + n], banks[b][:, 0:n], [("ps", b)], [("x", kc, cti)])

        if stop == 'x':
            return nc, P
        for l in range(NL):
            last_layer = (l == NL - 1)
            P.barrier()
            for j in range(8):
                h_, g_ = j // 4, j % 4
                idx = l * 8 + j
                TS(sinkrow_p[0:1, h_, g_ * 128:(g_ + 1) * 128], onesf[0:1, 0:128],
                   sinkex[0:1, idx:idx + 1], ALU.mult, ["sinkex", "onesf"], ["sinkrow_p"])
                if gi == 1:
                    TS(sinkrow_s[0:1, j * 64:(j + 1) * 64], onesf[0:1, 0:NS],
                       sinkex[0:1, idx:idx + 1], ALU.mult, ["sinkex", "onesf"], ["sinkrow_s"])
            if gi == 0 and l == 0:
                issue_ptok(0, 0)
            for cti, c0, n in cts:
                for k2 in range(2):
                    b = nbank()
                    if cti < 2:
                        for tb in range(4):
                            MM(banks[b][:, tb * 128:(tb + 1) * 128], ptok[:, cti * 4 + tb, k2 * 128:(k2 + 1) * 128],
                               identb[:, :], True, True, ["ptok", "identb"], b)
                    else:
                        MM(banks[b][:, 0:NS], ptok[0:NS, 8, k2 * 128:(k2 + 1) * 128], identb[0:NS, 0:NS],
                           True, True, ["ptok", "identb"], b)
                    COPY(pT[:, k2, c0:c0 + n], banks[b][:, 0:n], [("ps", b)], [("pT", k2, cti)])
            if l + 1 < NL:
                issue_ptok(gi, l + 1)
            elif gi + 1 < NG:
                issue_ptok(gi + 1, 0)
            if stop == 'L0':
                return nc, P
            if gi == 1:
                if 'cache' not in SKIP:
                    DMA("pool", ckt[:], ck_d[l].rearrange("b k f -> k b f"), [], ["ckt"], "ckt", alias=True)
                    for dup in range(2):
                        for h_ in range(2):
                            DMA("pool", cvd[:, :, h_, dup * 64:(dup + 1) * 64],
                                cv_d[l, :, :, h_ * 64:(h_ + 1) * 64].rearrange("b k d -> k b d"), [], ["cvd"], "cvd", alias=True)
                if 'd2d' not in SKIP:
                    DMA("pool", ks_d[l, :, 0:124, :], ck_d[l, :, 4:128, :], [], [], "d2d")
                    DMA("pool", vs_d[l, :, 0:124, :], cv_d[l, :, 4:128, :], [], [], "d2d")
                    DMA("pool", pos_d[l, :, 0:11, :], sp_d[l, :, 4:15, :], [], [], "d2d")
                for sw in range(2):
                    COPY(kT[sw][:, 0:128], kprev[:, l * 2 + sw, :], ["kprev"], [("kT", sw, "m")])
                COPY(vd[:, 0, :, :], vprev[:, l, :, :], ["vprev"], [("vd", 0)])
                COPY(ut[:, :, 0:16], uprev[:, 4 * l:4 * l + 4, :], ["uprev"], [("u", "m")])
                for sw in range(2):
                    MEMSET(kT[sw][:, 128 + TT:128 + TT + 64], 0.0, [("kT", sw, "pad")])
                MEMSET(vnd[64:128, :, :], 0.0, ["vnd"])
            else:
                MEMSET(ut[:, :, 0:16], 0.0, [("u", "m")])

            if stop == 'L1':
                return nc, P
            pre_norm(gi, l, 0)
            if stop == 'A0':
                return nc, P
            for ui in range(5):
                s, wv = next_unit("in%d" % ui)
                wres = ("ws", s)
                for half in range(2):
                    oc = 2 * ui + half
                    if oc == 5:
                        continue
                    for cti, c0, n in cts:
                        if oc == 4 and cti == 2 and 'k2' in SKIP:
                            continue
                        b = nbank()
                        for kc in range(NCH):
                            MM(banks[b][:, 0:n], wv[:, kc, half * 128:(half + 1) * 128], hn[:, kc, c0:c0 + n],
                               kc == 0, kc == NCH - 1, [wres, ("hn", kc, cti)], b)
                        if oc < 4:
                            COPY(qT[:, oc, c0:c0 + n], banks[b][:, 0:n], [("ps", b)], [("q", oc, cti)])
                        elif oc == 4:
                            COPY(kT[0][:, 128 + c0:128 + c0 + n], banks[b][:, 0:n], [("ps", b)], [("kT", 0, cti)])
                            b2 = nbank()
                            for hh in range(2):
                                for kc in range(NCH):
                                    MM(banks[b2][64 * hh:64 * hh + 64, 0:n],
                                       wv[:, kc, 64 * (1 - hh):64 * (1 - hh) + 64], hn[:, kc, c0:c0 + n],
                                       kc == 0, kc == NCH - 1, [wres, ("hn", kc, cti)], b2)
                            COPY(kT[1][:, 128 + c0:128 + c0 + n], banks[b2][:, 0:n], [("ps", b2)], [("kT", 1, cti)])
                        else:
                            g = oc - 6
                            if cti < 2:
                                COPY(ut[:, g, 16 + c0:16 + c0 + n], banks[b][:, 0:n], [("ps", b)], [("u", g, cti)])
                            else:
                                COPY(uext[:, g, :, 15:19], banks[b][:, 0:NS].rearrange("p (b s) -> p b s", b=NB),
                                     [("ps", b)], [("uext", g)])
                if ui == 2:
                    for cti, c0, n in cts:
                        if cti < 2:
                            for tb in range(4):
                                blk = cti * 4 + tb
                                lastb = (gi == 1 and blk == 7 and 'lastb' not in SKIP)
                                b = nbank()
                                cA = 0 if lastb else 128
                                for kc in range(NCH):
                                    MM(banks[b][:, cA:256], hn[:, kc, c0 + tb * 128:c0 + (tb + 1) * 128],
                                       wv[:, kc, cA:256], kc == 0, kc == NCH - 1, [wres, ("hn", kc, cti)], b)
                                vsrc = banks[b][:, 128:256].rearrange("p (h d) -> p h d", h=2)
                                COPY(vd[:, 1 + blk, :, 0:64], vsrc, [("ps", b)], [("vd", 1 + blk)], eng="act")
                                COPY(vd[:, 1 + blk, :, 64:128], vsrc, [("ps", b)], [("vd", 1 + blk)], eng="dve")
                                if lastb:
                                    COPY(kvst[:, :], banks[b][:, 0:256], [("ps", b)], ["kvst"])
                                    if 'tokout' not in SKIP:
                                        DMA("sp", kp_d[l], kvst[:, 0:128], ["kvst"], [], "kvst", alias=True)
                                        DMA("sp", vp_d[l], kvst[:, 128:256], ["kvst"], [], "kvst", alias=True)
                        elif 'stok' not in SKIP:
                            b = nbank()
                            for kc in range(NCH):
                                MM(banks[b][0:NS, 0:256], hn[:, kc, c0:c0 + NS], wv[:, kc, 0:256],
                                   kc == 0, kc == NCH - 1, [wres, ("hn", kc, cti)], b)
                            vsrc = banks[b][0:NS, 128:256].rearrange("p (h d) -> p h d", h=2)
                            COPY(vnd[0:NS, :, 0:64], vsrc, [("ps", b)], ["vnd"], eng="act")
                            COPY(vnd[0:NS, :, 64:128], vsrc, [("ps", b)], ["vnd"], eng="dve")
                            COPY(kvs[:, :], banks[b][0:NS, 0:256], [("ps", b)], ["kvs"])
                            if 'tokout' not in SKIP:
                                DMA("sp", ks_d[l, :, 124:128, :], kvs[:, 0:128], ["kvs"], [], "kvs", alias=True)
                                DMA("sp", vs_d[l, :, 124:128, :], kvs[:, 128:256], ["kvs"], [], "kvs", alias=True)
                if ui >= 3 and gi == 1:
                    hcol = (ui - 3) * 256
                    b = nbank()
                    for kc in range(NCH):
                        MM(banks[b][:, 0:256], hn[:, kc, 896:1024], wv[:, kc, 0:256], kc == 0, kc == NCH - 1,
                           [wres, ("hn", kc, 1)], b)
                    COPY(ust[:, hcol:hcol + 256], banks[b][:, 0:256], [("ps", b)], ["ust"])
                    b = nbank()
                    for kc in range(NCH):
                        MM(banks[b][0:NS, 0:256], hn[:, kc, GT:GT + NS], wv[:, kc, 0:256], kc == 0, kc == NCH - 1,
                           [wres, ("hn", kc, 2)], b)
                    COPY(usts[:, hcol:hcol + 256], banks[b][0:NS, 0:256], [("ps", b)], ["usts"])
                    if ui == 4:
                        if 'tokout' not in SKIP:
                            DMA("sp", pop_d[l], ust[113:128, :], ["ust"], [], "ust", alias=True)
                            DMA("sp", pos_d[l, :, 11:15, :], usts[:, :], ["usts"], [], "usts", alias=True)
                release_unit()
                if stop == 'Au%d' % ui:
                    return nc, P

            if stop == 'A':
                return nc, P
            if gi == 0 and NG > 1:
                for sw in range(2):
                    COPY(kprev[:, l * 2 + sw, :], kT[sw][:, 128 + 896:128 + 1024], [("kT", sw, 1)], ["kprev"])
                COPY(vprev[:, l, :, :], vd[:, 8, :, :], [("vd", 8)], ["vprev"])
                COPY(uprev[:, 4 * l:4 * l + 4, :], ut[:, :, GT:GT + 16], [("u", g, 1) for g in range(4)], ["uprev"])

            if gi == 1:
                P.barrier()
            ps_, pwv = next_unit("pool")
            pwres = ("ws", ps_)

            def pool_prompt_gen():
                for cti, c0, n in cts:
                    if cti == 2:
                        continue
                    base = 16 + c0
                    for g in range(4):
                        src = lambda a, b_, g=g: ut[:, g, a:b_]
                        src_res = [("u", "m")] + [("u", g, cc) for cc in range(cti + 1)]
                        lo = 0
                        ti = 0
                        step = 1
                        for lev in range(g + 1):
                            nlo = lo - 0
                            rem = (WINS[g] - 1) - (2 * step - 1)
                            o0 = base - rem
                            dst = ptmp[ti]
                            off = 16 - rem
                            width = 512 + rem
                            VTT(dst[:, off:off + width], src(o0, o0 + width), src(o0 - step, o0 - step + width), ALU.add,
                               src_res, [("ptmp", ti)])
                            tsel = ti
                            src = lambda a, b_, dst=dst, base=base: dst[:, a - (base - 16):b_ - (base - 16)]
                            src_res = [("ptmp", tsel)]
                            ti ^= 1
                            step *= 2
                        tot = src(base, base + 512)
                        STT(dl[:, g, :], tot, 1.0 / WINS[g], ut[:, g, base:base + 512], ALU.mult, ALU.subtract,
                            src_res + [("u", g, cti)], [("dl", g)])
                        if gi == 0 and cti == 0:
                            t16 = src(base, base + 16)
                            other = ptmp[ti]
                            VTT(other[:, 0:16], t16, invcnt[:, g, :], ALU.mult, src_res + ["invcnt"], [("ptmp", ti)])
                            VTT(dl[:, g, 0:16], other[:, 0:16], ut[:, g, base:base + 16], ALU.subtract,
                               [("ptmp", ti), ("u", g, cti)], [("dl", g)])
                        b = nbank()
                        MM(banks[b][:, :], pwv[:, g, :], dl[:, g, :], True, True, [pwres, ("dl", g)], b)
                        ACT(am[:, 4 + g, c0:c0 + 512], banks[b][:, :], AF.Copy, [("ps", b), "pscale"],
                            [("hn", 4 + g, cti)], scale=pscale[:, g, l:l + 1])
                        yield

            pgen = pool_prompt_gen()
            def khead(h, j):
                bq = 64 * (j % 2)
                if bq == 64 * h:
                    return 0, bq
                return 1, bq

            pt_i = [0]
            ptmap = {}
            for i in range(-1 if gi == 1 else 0, 8):
                kc0 = 128 * (i + 1)
                qlo, qhi = max(i, 0), min(i + 1, 7)
                nq = 128 * (qhi - qlo + 1)
                kcti = 0 if i < 4 else 1
                for h in range(2):
                    pi = pt_i[0] % NPT
                    pt_i[0] += 1
                    ptmap[(i, h)] = pi
                    if nq == 256:
                        msk, mname = maskp2, "maskp2"
                    elif i < 0:
                        msk, mname = masko, "masko"
                    else:
                        msk, mname = maskd, "maskd"
                    ncol = 2 * nq
                    bks = (nbank(), nbank())
                    for bk in bks:
                        MM(banks[bk][:, 0:ncol], identb[:, :], msk[:, 0:ncol], True, False, ["identb", mname], bk)
                    for g in range(4):
                        e_, a_ = g % 2, g // 2
                        bk = bks[e_]
                        j = 4 * h + g
                        sw, bq = khead(h, j)
                        kres = ("kT", sw, "m") if i < 0 else ("kT", sw, kcti)
                        MM(banks[bk][:, a_ * nq:(a_ + 1) * nq], kT[sw][bq:bq + 64, kc0:kc0 + 128],
                           qT[bq:bq + 64, j // 2, qlo * 128:qlo * 128 + nq], False, a_ == 1,
                           [kres, ("q", j // 2, qlo // 4), ("q", j // 2, qhi // 4)], bk)
                    for e_, bk in enumerate(bks):
                        ACT(ptb[pi][:, :, :].rearrange("p (a e) q -> p a e q", e=2)[:, :, e_, 0:nq],
                            banks[bk][:, 0:ncol].rearrange("p (a q) -> p a q", a=2),
                            AF.Exp, [("ps", bk)], [("pt", pi)], scale=0.125)
                if i >= 0:
                    qb = i
                    qcti = qb // 4
                    for h in range(2):
                        srcs = []
                        if qb >= 1 or gi == 1:
                            pprev = ptmap[(qb - 1, h)]
                            nqprev = 128 if (qb - 1) < 0 else 256
                            srcs.append((pprev, qb, ptb[pprev][:, :, 128:256] if nqprev == 256 else ptb[pprev][:, :, 0:128]))
                        pcur = ptmap[(qb, h)]
                        srcs.append((pcur, qb + 1, ptb[pcur][:, :, 0:128]))
                        bo = nbank()
                        bd = nbank()
                        for si, (pi, vblk, rhs) in enumerate(srcs):
                            MM(banks[bo][:, :], vd[:, vblk, h, :], rhs, si == 0, si == len(srcs) - 1,
                               [("vd", vblk), ("pt", pi)], bo)
                        for si, (pi, vblk, rhs) in enumerate(srcs):
                            MM(banks[bd][:, :], onesb[:, :], rhs, si == 0, False, ["onesb", ("pt", pi)], bd)
                        MM(banks[bd][:, :], onesb[0:1, :], sinkrow_p[0:1, h, :], False, True,
                           ["onesb", "sinkrow_p"], bd)
                        r = h
                        ACT(recip[r][:, :], banks[bd][:, :], AF.Ln, [("ps", bd)], [("recip", r)])
                        ACT(recip[r][:, :], recip[r][:, :], AF.Exp, [("recip", r)], [("recip", r)], scale=-1.0)
                        for e_ in range(2):
                            o_ap = am[64 * e_:64 * e_ + 64, 2 * h:2 * h + 2, qb * 128:(qb + 1) * 128]
                            i0 = banks[bo][64 * e_:64 * e_ + 64, :].rearrange("p (a e q) -> p a e q", a=2, e=2)[:, :, e_, :]
                            i1 = recip[r][64 * e_:64 * e_ + 64, :].rearrange("p (a e q) -> p a e q", a=2, e=2)[:, :, e_, :]
                            VTT(o_ap, i0, i1, ALU.mult, [("ps", bo), ("recip", r)],
                               [("hn", 2 * h, qcti), ("hn", 2 * h + 1, qcti)])
                    next(pgen, None)

            for _ in pgen:
                pass
            if stop == 'B':
                return nc, P
            if gi == 1:
                cs = GT
                for b4 in range(NB // 4):
                    for sw in range(2):
                        b = nbank()
                        for bi in range(4):
                            bb = b4 * 4 + bi
                            if sw == 0:
                                MM(banks[b][:, bi * 128:(bi + 1) * 128], ckt[:, bb, :], identb[:, :], True, True,
                                   ["ckt", "identb"], b)
                            else:
                                for hh in range(2):
                                    MM(banks[b][64 * hh:64 * hh + 64, bi * 128:(bi + 1) * 128],
                                       ckt[:, bb, 64 * (1 - hh):64 * (1 - hh) + 64], identb[:, :], True, True,
                                       ["ckt", "identb"], b)
                        COPY(ckT[sw][:, b4 * 4:b4 * 4 + 4, :], banks[b][:, :].rearrange("p (b k) -> p b k", b=4),
                             [("ps", b)], [("ckT", sw)])
                if stop == 'B2a':
                    return nc, P
                bse = (nbank(), nbank())
                for bk in bse:
                    MM(banks[bk][:, 0:256], identb[:, :], masksc[:, 0:256], True, False, ["identb", "masksc"], bk)
                for bb in range(NB):
                    for j in range(8):
                        sw, bq = khead(j // 4, j)
                        c_ = (j // 2) * 64 + bb * 4
                        bk = bse[j % 2]
                        MM(banks[bk][:, c_:c_ + 4], ckT[sw][bq:bq + 64, bb, :],
                           qT[bq:bq + 64, j // 2, cs + 4 * bb:cs + 4 * bb + 4], False, (bb == NB - 1 and j >= 6),
                           [("ckT", sw), ("q", j // 2, 2)], bk)
                for e_, bk in enumerate(bse):
                    ACT(ptc[:, :].rearrange("p (a e c) -> p a e c", a=4, e=2)[:, :, e_, :],
                        banks[bk][:, 0:256].rearrange("p (a c) -> p a c", a=4), AF.Exp, [("ps", bk)], ["ptc"], scale=0.125)
                if stop == 'B2b':
                    return nc, P
                bne = (nbank(), nbank())
                for bk in bne:
                    MM(banks[bk][:, 0:256], identb[:, :], masksn[:, 0:256], True, False, ["identb", "masksn"], bk)
                for bb in range(NB):
                    for j in range(8):
                        sw, bq = khead(j // 4, j)
                        c_ = (j // 2) * 64 + bb * 4
                        bk = bne[j % 2]
                        MM(banks[bk][:, c_:c_ + 4], kT[sw][bq:bq + 64, 128 + cs:128 + cs + 128],
                           qT[bq:bq + 64, j // 2, cs + 4 * bb:cs + 4 * bb + 4], False, (bb == NB - 1 and j >= 6),
                           [("kT", sw, 2), ("kT", sw, "pad"), ("q", j // 2, 2)], bk)
                for e_, bk in enumerate(bne):
                    ACT(ptn[:, :].rearrange("p (a e c) -> p a e c", a=4, e=2)[:, :, e_, :],
                        banks[bk][:, 0:256].rearrange("p (a c) -> p a c", a=4), AF.Exp, [("ps", bk)], ["ptn"], scale=0.125)
                if stop == 'B2c':
                    return nc, P
                bo = nbank()
                for h in range(2):
                    MM(banks[bo][:, h * 256:(h + 1) * 256], vnd[:, h, :], ptn[:, h * 256:(h + 1) * 256], h == 0, False,
                       ["vnd", "ptn"], bo)
                for bb in range(NB):
                    for j in range(8):
                        c_ = j * 64 + bb * 4
                        MM(banks[bo][:, c_:c_ + 4], cvd[:, bb, j // 4, :], ptc[:, c_:c_ + 4], False,
                           (bb == NB - 1 and j == 7), ["cvd", "ptc"], bo)
                bd = nbank()
                MM(banks[bd][:, :], onesb[:, :], ptn[:, :], True, False, ["onesb", "ptn"], bd)
                MM(banks[bd][:, :], onesb[:, :], ptc[:, :], False, False, ["onesb", "ptc"], bd)
                MM(banks[bd][:, :], onesb[0:1, :], sinkrow_s[0:1, :], False, True, ["onesb", "sinkrow_s"], bd)
                ACT(recip[0][:, :], banks[bd][:, :], AF.Ln, [("ps", bd)], [("recip", 0)])
                ACT(recip[0][:, :], recip[0][:, :], AF.Exp, [("recip", 0)], [("recip", 0)], scale=-1.0)
                for j in range(8):
                    e_ = j % 2
                    VTT(am[64 * e_:64 * e_ + 64, j // 2, cs:cs + NS], banks[bo][64 * e_:64 * e_ + 64, j * 64:(j + 1) * 64],
                        recip[0][64 * e_:64 * e_ + 64, j * 64:(j + 1) * 64], ALU.mult,
                        [("ps", bo), ("recip", 0)], [("hn", j // 2, 2)])

            if stop == 'B2':
                return nc, P
            if gi == 1:
                for half in range(2):
                    DMA("sp", stp[:], sp_d[l, 8 * half:8 * half + 8].rearrange("b r f -> (b r) f"),
                        [], ["stp"], "stp", alias=True)
                    for g in range(4):
                        b = nbank()
                        TR(banks[b][:, 0:120], stp[0:120, g * 128:(g + 1) * 128], identf[0:120, 0:120],
                           ["stp", "identf"], b)
                        COPY(uext[:, g, 8 * half:8 * half + 8, 0:15],
                             banks[b][:, 0:120].rearrange("p (b r) -> p b r", b=8), [("ps", b)], [("uext", g)])
                b = nbank()
                for g in range(4):
                    src = lambda a, b_, g=g: uext[:, g, :, a:b_]
                    src_res = [("uext", g)]
                    ti = 0
                    step = 1
                    for lev in range(g + 1):
                        rem = (WINS[g] - 1) - (2 * step - 1)
                        o0 = 15 - rem
                        width = 4 + rem
                        dst = stmp[ti]
                        VTT(dst[:, :, o0:o0 + width], src(o0, o0 + width), src(o0 - step, o0 - step + width), ALU.add,
                           src_res, [("stmp", ti)])
                        src = lambda a, b_, dst=dst: dst[:, :, a:b_]
                        src_res = [("stmp", ti)]
                        ti ^= 1
                        step *= 2
                    tot = src(15, 19)
                    STT(dls[:, g, :].rearrange("p (b s) -> p b s", b=NB), tot, 1.0 / WINS[g], uext[:, g, :, 15:19],
                        ALU.mult, ALU.subtract, src_res + [("uext", g)], [("dls", g)])
                    MM(banks[b][:, g * NS:(g + 1) * NS], pwv[:, g, :], dls[:, g, :], True, True, [pwres, ("dls", g)], b)
                for g in range(4):
                    ACT(am[:, 4 + g, GT:GT + NS], banks[b][:, g * NS:(g + 1) * NS], AF.Copy, [("ps", b), "pscale"],
                        [("hn", 4 + g, 2)], scale=pscale[:, g, l:l + 1])
            release_unit()

            if stop == 'C':
                return nc, P
            P.barrier()
            def branch_out(unit_names, kchunks, src_tile, src_name, nidx, nhalf):
                pend = []
                bank_mod[0] = 5

                def flush(keep):
                    while len(pend) > keep:
                        pend.pop(0)()

                for ui, uname in enumerate(unit_names):
                    s_, wv = next_unit(uname)
                    wres = ("ws", s_)
                    for half in range(nhalf):
                        oc = ui * nhalf + half
                        for cti, c0, n in cts:
                            b = nbank()
                            for kc in range(kchunks):
                                MM(banks[b][:, 0:n], wv[:, kc, half * 128:(half + 1) * 128], src_tile[:, kc, c0:c0 + n],
                                   kc == 0, kc == kchunks - 1, [wres, (src_name, kc, cti)], b)
                            flush(1)
                            si = sq_i[0] % 3
                            sq_i[0] += 1
                            ACT(mixg[:, oc, c0:c0 + n], banks[b][:, 0:n], AF.Copy, [("ps", b), "gains"],
                                [("mg", oc, cti)], scale=gain_ap(nidx, l, oc))
                            ACT(sqtmp[si][:, 0:n], banks[b][:, 0:n], AF.Square, [("ps", b)], [("sqt", si)])
                            pend.append(lambda oc=oc, cti=cti, n=n, si=si: MM(
                                banks[5 + cti][:, 0:n], meanb[:, :], sqtmp[si][:, 0:n], oc == 0, oc == NCH - 1,
                                [("sqt", si), "meanb"], 5 + cti))
                    release_unit()
                    if stop == 'D0':
                        raise _Stop()
                if stop == 'D1':
                    raise _Stop()
                flush(0)
                bank_mod[0] = 8
                if stop == 'D2':
                    raise _Stop()
                for cti, c0, n in cts:
                    r = rstd_from_ms(cti, n)
                    rb = rstd[r][:, 0:n].unsqueeze(1).broadcast_to([128, 4, n])
                    for hf in range(2):
                        ocs = range(4 * hf, 4 * hf + 4)
                        VTT(mixg[:, 4 * hf:4 * hf + 4, c0:c0 + n], mixg[:, 4 * hf:4 * hf + 4, c0:c0 + n], rb, ALU.mult,
                           [("mg", oc, cti) for oc in ocs] + [("rstd", r)], [("mg", oc, cti) for oc in ocs])
                    for hf in range(2):
                        ocs = range(4 * hf, 4 * hf + 4)
                        VTT(x[:, 4 * hf:4 * hf + 4, c0:c0 + n], x[:, 4 * hf:4 * hf + 4, c0:c0 + n],
                           mixg[:, 4 * hf:4 * hf + 4, c0:c0 + n], ALU.add,
                           [("mg", oc, cti) for oc in ocs] + [("x", oc, cti) for oc in ocs], [("x", oc, cti) for oc in ocs])

            branch_out(["out%d" % i for i in range(4)], NCH, am, "hn", 1, 2)

            if stop == 'D':
                return nc, P
            pre_norm(gi, l, 2)
            for i in range(11):
                sg_, wg = next_unit("gate%d" % i)
                su_, wu = next_unit("up%d" % i)
                for half in range(2):
                    c = 2 * i + half
                    for cti, c0, n in cts:
                        bg = nbank()
                        for kc in range(NCH):
                            MM(banks[bg][:, 0:n], wg[:, kc, half * 128:(half + 1) * 128], hn[:, kc, c0:c0 + n],
                               kc == 0, kc == NCH - 1, [("ws", sg_), ("hn", kc, cti)], bg)
                        bu = nbank()
                        for kc in range(NCH):
                            MM(banks[bu][:, 0:n], wu[:, kc, half * 128:(half + 1) * 128], hn[:, kc, c0:c0 + n],
                               kc == 0, kc == NCH - 1, [("ws", su_), ("hn", kc, cti)], bu)
                        r = (c * 3 + cti) & 1
                        ACT(sgt[r][:, 0:n], banks[bg][:, 0:n], AF.Silu, [("ps", bg)], [("sg", r)])
                        VTT(hT[:, c, c0:c0 + n], sgt[r][:, 0:n], banks[bu][:, 0:n], ALU.mult,
                           [("sg", r), ("ps", bu)], [("h", c, cti)])
                release_unit()
                release_unit()
            branch_out(["down%d" % i for i in range(8)], NFC, hT, "h", 3, 1)

            if stop == 'E':
                return nc, P
            for cti, c0, n in cts:
                for half in range(2):
                    ACT(hn[:, 4 * half:4 * half + 4, c0:c0 + n], x[:, 4 * half:4 * half + 4, c0:c0 + n], AF.Copy,
                        [("x", kc, cti) for kc in range(4 * half, 4 * half + 4)],
                        [("hn", kc, cti) for kc in range(4 * half, 4 * half + 4)])
            for ui in range(4):
                se_, we = next_unit("ple%d" % ui)
                s, wv = next_unit("pg%d" % ui)
                for half in range(2):
                    oc = 2 * ui + half
                    for cti, c0, n in cts:
                        bg = nbank()
                        for kc in range(NCH):
                            MM(banks[bg][:, 0:n], wv[:, kc, half * 128:(half + 1) * 128], hn[:, kc, c0:c0 + n],
                               kc == 0, kc == NCH - 1, [("ws", s), ("hn", kc, cti)], bg)
                        be = nbank()
                        for k2 in range(2):
                            MM(banks[be][:, 0:n], we[:, k2, half * 128:(half + 1) * 128], pT[:, k2, c0:c0 + n],
                               k2 == 0, k2 == 1, [("ws", se_), ("pT", k2, cti)], be)
                        r = (oc + cti) & 1
                        ACT(sgt[r][:, 0:n], banks[bg][:, 0:n], AF.Sigmoid, [("ps", bg)], [("sg", r)])
                        VTT(sgt[r][:, 0:n], sgt[r][:, 0:n], banks[be][:, 0:n], ALU.mult,
                           [("sg", r), ("ps", be)], [("sg", r)])
                        VTT(x[:, oc, c0:c0 + n], x[:, oc, c0:c0 + n], sgt[r][:, 0:n], ALU.add,
                           [("sg", r), ("x", oc, cti)], [("x", oc, cti)])
                release_unit()
                release_unit()

        P.barrier()
        oi = 0
        for cti, c0, n in cts:
            ntb = 4 if cti < 2 else 1
            for tb in range(ntb):
                np_ = 128 if cti < 2 else NS
                yb = oi & 1
                oi += 1
                for hf in range(2):
                    b = nbank()
                    for k4 in range(4):
                        kc = hf * 4 + k4
                        TR(banks[b][0:np_, k4 * 128:(k4 + 1) * 128], x[:, kc, c0 + tb * 128:c0 + tb * 128 + np_],
                           identf[:, :], [("x", kc, cti), "identf"], b)
                    COPY(yst[yb][0:np_, hf * 512:(hf + 1) * 512], banks[b][0:np_, :], [("ps", b)], [("yst", yb, hf)])
                if cti < 2:
                    r0 = tok0 + c0 + tb * 128
                    DMA("sp", yp_d[r0:r0 + 128, :], yst[yb][:, :], [("yst", yb, 0), ("yst", yb, 1)], [], "yst%d" % yb, alias=True)
                else:
                    DMA("sp", ys_d, yst[yb][0:NS, :], [("yst", yb, 0), ("yst", yb, 1)], [], "yst%d" % yb, alias=True)

      except _Stop:
        return nc, P
    return nc, P


def finalize(nc, P):
    esem = {}
    for e in ("pe", "act", "dve"):
        esem[e] = nc.alloc_semaphore("sem_" + e)
        cnt = 0
        for op in P.q[e]:
            if op.sig:
                cnt += 1
                op.semval = cnt
    dsem = {g: nc.alloc_semaphore("dsem_" + g) for g in P.dma_cnt}

    def emit(name, e):
        for op in P.q[name]:
            for key, val, p in op.waits:
                if isinstance(key, tuple):
                    e.wait_ge(dsem[key[1]], val)
                else:
                    e.wait_ge(esem[key], p.semval)
            ins = op.fn(e)
            if op.dma is not None:
                ins.then_inc(dsem[op.dma], 16)
            elif op.sig:
                ins.then_inc(esem[name], 1)
        if name == "sp":
            for g, c in P.dma_cnt.items():
                e.wait_ge(dsem[g], c * 16)

    with nc.Block() as block:
        @block.tensor
        def _(e):
            emit("pe", e)

        @block.scalar
        def _(e):
            emit("act", e)

        @block.vector
        def _(e):
            emit("dve", e)

        @block.gpsimd
        def _(e):
            emit("pool", e)

        @block.sync
        def _(e):
            emit("sp", e)
    return nc


def host_consts():
    bf = ml_dtypes.bfloat16
    c = {}
    c["c_identb"] = np.eye(128, dtype=np.float32).astype(bf)
    c["c_identf"] = np.eye(128, dtype=np.float32)
    c["c_onesb"] = np.ones((128, 128), np.float32).astype(bf)
    c["c_meanb"] = np.full((128, 128), 1.0 / D, np.float32).astype(bf)
    k = np.arange(128)[:, None]
    q2 = np.arange(256)[None, :]
    band = np.where((q2 >= k) & (q2 < k + 128), 0.0, NEG).astype(np.float32)
    c["c_maskp2"] = np.concatenate([band, band], axis=1).astype(bf)
    qd = np.arange(128)[None, :]
    diag = np.where(k <= qd, 0.0, NEG).astype(np.float32)
    offd = np.where(qd < k, 0.0, NEG).astype(np.float32)
    c["c_maskd"] = np.tile(diag, (1, 4)).astype(bf)
    c["c_masko"] = np.tile(offd, (1, 4)).astype(bf)
    s_ = np.tile(np.arange(4), NB * 8)[None, :]
    c["c_masksc"] = np.where(k > s_, 0.0, NEG).astype(np.float32).astype(bf)
    bq = np.tile(np.repeat(np.arange(NB), 4), 8)[None, :]
    kb = (np.arange(NS) // 4)[:, None]
    ks = (np.arange(NS) % 4)[:, None]
    msn = np.full((128, 256), NEG, np.float32)
    msn[0:NS] = np.where((kb == bq) & (ks <= s_), 0.0, NEG)[:, 0:256]
    c["c_masksn"] = msn.astype(bf)
    inv = np.zeros((128, 4, 16), np.float32)
    for g, w in enumerate(WINS):
        for t in range(16):
            inv[:, g, t] = 1.0 / min(w, t + 1)
    c["c_invcnt"] = inv.reshape(128, 64)
    return c


_CACHE = {}


def kernel(x_prompt, x_sample, p_prompt, p_sample, cache_k, cache_v, state_pool,
           norm_mix_pre, norm_mix_post, norm_ffn_pre, norm_ffn_post, w_in, w_out,
           attn_sinks, w_pool, pool_scale, w_gate, w_up, w_down, w_ple, w_ple_gate):
    f = lambda a: np.ascontiguousarray(np.asarray(a, dtype=np.float32))
    if "nc" not in _CACHE:
        nc, P = build_program()
        finalize(nc, P)
        _CACHE["nc"] = nc
    nc = _CACHE["nc"]
    consts = host_consts()
    shared = {
        "n_mix_pre": f(norm_mix_pre), "n_mix_post": f(norm_mix_post), "n_ffn_pre": f(norm_ffn_pre),
        "n_ffn_post": f(norm_ffn_post), "w_in": f(w_in), "w_out": f(w_out),
        "sinks": f(attn_sinks).reshape(1, DEPTH * 8), "w_pool": f(w_pool), "pool_scale": f(pool_scale),
        "w_gate": f(w_gate), "w_up": f(w_up), "w_down": f(w_down), "w_ple": f(w_ple), "w_ple_gate": f(w_ple_gate),
    }
    shared.update(consts)
    xp, xs, pp, psm = f(x_prompt), f(x_sample), f(p_prompt), f(p_sample)
    ck, cv, spool = f(cache_k), f(cache_v), f(state_pool)
    in_maps = []
    for c in range(8):
        m = dict(shared)
        sl = slice(NB * c, NB * (c + 1))
        m["xp"] = xp[c]
        m["xs"] = xs[sl].reshape(NS, D)
        m["pp"] = np.ascontiguousarray(pp[:, c])
        m["psm"] = np.ascontiguousarray(psm[:, sl]).reshape(DEPTH, NS, 256)
        m["ck"] = np.ascontiguousarray(ck[:, sl]).reshape(DEPTH, NB, 128, 128)
        m["cv"] = np.ascontiguousarray(cv[:, sl]).reshape(DEPTH, NB, 128, 128)
        m["spool"] = np.ascontiguousarray(spool[:, sl])
        in_maps.append(m)
    res = run_bass_kernel_spmd(nc, in_maps, core_ids=list(range(8)))
    R = res.results
    y_prompt = np.stack([R[c]["y_prompt"] for c in range(8)], 0)
    y_sample = np.concatenate([R[c]["y_sample"].reshape(NB, 4, D) for c in range(8)], 0)
    k_prompt = np.stack([R[c]["k_prompt"].reshape(DEPTH, 128, 2, 64) for c in range(8)], 1)
    v_prompt = np.stack([R[c]["v_prompt"].reshape(DEPTH, 128, 2, 64) for c in range(8)], 1)
    pool_prompt = np.stack([R[c]["pool_prompt"] for c in range(8)], 1)
    k_sample = np.concatenate([R[c]["k_sample"].reshape(DEPTH, NB, 128, 2, 64) for c in range(8)], 1)
    v_sample = np.concatenate([R[c]["v_sample"].reshape(DEPTH, NB, 128, 2, 64) for c in range(8)], 1)
    pool_sample = np.concatenate([R[c]["pool_sample"] for c in range(8)], 1)
    return tuple(np.ascontiguousarray(a.astype(np.float32)) for a in
                 (y_prompt, y_sample, k_prompt, v_prompt, pool_prompt, k_sample, v_sample, pool_sample))
```
